# Optimizing a Trainium2 kernel written in Bass

```python
import jax, jax.numpy as jnp
from jax import lax
import numpy as np

D_MODEL = 1024
BATCH = 8
SEQ = 4096
DEPTH = 4

N_MIXERS = 2
HEAD_DIM = 64
N_HEADS = D_MODEL // HEAD_DIM
MIX_WIDTH = N_HEADS * HEAD_DIM
N_KV_A = max(1, N_HEADS // 8)
N_KV_B = max(1, N_HEADS // 4)
WINDOW = 128
MOBA_BLOCK = 256
MOBA_TOPK = 3
MOBA_Q_CHUNK = 16
ROPE_THETA = 10000.0
NORM_EPS = 1e-5
NEG = -1e30
N_LAYERS_A = (DEPTH + 1) // 2
N_LAYERS_B = DEPTH // 2
IN_A = 2 * MIX_WIDTH + 2 * N_KV_A * HEAD_DIM
IN_B = 2 * MIX_WIDTH + 2 * N_KV_B * HEAD_DIM

kernel_name = "hybrid_swa_sink_moba_gated"


def rmsnorm(x, g):
    xf = x.astype(jnp.float32)
    y = xf * lax.rsqrt(jnp.mean(xf * xf, axis=-1, keepdims=True) + NORM_EPS)
    return (y * g.astype(jnp.float32)).astype(x.dtype)


def rope_tables(seq):
    pos = jnp.arange(seq, dtype=jnp.float32)
    inv = ROPE_THETA ** (-jnp.arange(0, HEAD_DIM, 2, dtype=jnp.float32) / HEAD_DIM)
    ang = pos[:, None] * inv[None, :]
    return jnp.cos(ang), jnp.sin(ang)


def apply_rope(x, cos, sin):
    half = HEAD_DIM // 2
    x1, x2 = x[..., :half], x[..., half:]
    c, s = cos[None, :, None, :], sin[None, :, None, :]
    out = jnp.concatenate([x1 * c - x2 * s, x2 * c + x1 * s], axis=-1)
    return out.astype(x.dtype)


def split_proj(h, w_in, n_kv):
    B, S, _ = h.shape
    proj = h @ w_in
    kvw = n_kv * HEAD_DIM
    q = proj[..., :MIX_WIDTH].reshape(B, S, N_HEADS, HEAD_DIM)
    k = proj[..., MIX_WIDTH:MIX_WIDTH + kvw].reshape(B, S, n_kv, HEAD_DIM)
    v = proj[..., MIX_WIDTH + kvw:MIX_WIDTH + 2 * kvw].reshape(B, S, n_kv, HEAD_DIM)
    z = proj[..., MIX_WIDTH + 2 * kvw:]
    return q, k, v, z


def sliding_window_sink_attention(q, k, v, sinks):
    B, S, H, d = q.shape
    kvh = k.shape[2]
    G = H // kvh
    nq = S // WINDOW
    scale = 1.0 / np.sqrt(d)
    qb = q.reshape(B, nq, WINDOW, kvh, G, d)
    pad = ((0, 0), (WINDOW, 0), (0, 0), (0, 0))
    kb = jnp.pad(k, pad).reshape(B, nq + 1, WINDOW, kvh, d)
    vb = jnp.pad(v, pad).reshape(B, nq + 1, WINDOW, kvh, d)
    kband = jnp.concatenate([kb[:, :-1], kb[:, 1:]], axis=2)
    vband = jnp.concatenate([vb[:, :-1], vb[:, 1:]], axis=2)
    s = jnp.einsum('bnikgd,bnjkd->bnkgij', qb, kband).astype(jnp.float32) * scale
    i = jnp.arange(WINDOW)[:, None]
    j = jnp.arange(2 * WINDOW)[None, :]
    band = (j > i) & (j <= i + WINDOW)
    not_pad = (jnp.arange(nq)[:, None, None] > 0) | (j[None] >= WINDOW)
    mask = band[None] & not_pad
    s = jnp.where(mask[None, :, None, None], s, NEG)
    sink = jnp.broadcast_to(sinks.astype(jnp.float32).reshape(kvh, G)[None, None, :, :, None, None],
                            s.shape[:-1] + (1,))
    p = jax.nn.softmax(jnp.concatenate([s, sink], axis=-1), axis=-1)[..., :-1]
    o = jnp.einsum('bnkgij,bnjkd->bnikgd', p.astype(v.dtype), vband)
    return o.reshape(B, S, H, d)


def moba_attention(q, k, v):
    B, S, H, d = q.shape
    kvh = k.shape[2]
    G = H // kvh
    nb = -(-S // MOBA_BLOCK)
    Sp = nb * MOBA_BLOCK
    scale = 1.0 / np.sqrt(d)
    pad = ((0, 0), (0, Sp - S), (0, 0), (0, 0))
    kblk = jnp.pad(k, pad).reshape(B, nb, MOBA_BLOCK, kvh, d).transpose(0, 3, 1, 2, 4)
    vblk = jnp.pad(v, pad).reshape(B, nb, MOBA_BLOCK, kvh, d).transpose(0, 3, 1, 2, 4)
    kmean = jnp.mean(kblk.astype(jnp.float32), axis=3)
    qg = q.reshape(B, S, kvh, G, d)
    gate = jnp.einsum('bskgd,bknd->bskgn', qg.astype(jnp.float32), kmean)
    qblock = jnp.arange(S) // MOBA_BLOCK
    past = jnp.arange(nb)[None, :] < qblock[:, None]
    gate = jnp.where(past[None, :, None, None, :], gate, NEG)
    kk = min(MOBA_TOPK, nb)
    _, idx = lax.top_k(gate, kk)
    nc = S // MOBA_Q_CHUNK
    q_c = qg.reshape(B, nc, MOBA_Q_CHUNK, kvh, G, d).transpose(1, 0, 2, 3, 4, 5)
    idx_c = idx.reshape(B, nc, MOBA_Q_CHUNK, kvh, G, kk).transpose(1, 0, 2, 3, 4, 5)
    bi = jnp.arange(B)[:, None, None, None, None]
    ki = jnp.arange(kvh)[None, None, :, None, None]

    def chunk(args):
        ci, qc, ic = args
        t0 = ci * MOBA_Q_CHUNK
        t = t0 + jnp.arange(MOBA_Q_CHUNK)
        cblk = t0 // MOBA_BLOCK
        ksel = kblk[bi, ki, ic]
        vsel = vblk[bi, ki, ic]
        s_sel = jnp.einsum('bqkgd,bqkgsnd->bqkgsn', qc, ksel).astype(jnp.float32) * scale
        slot_ok = jnp.arange(kk) < cblk
        s_sel = jnp.where(slot_ok[:, None], s_sel, NEG)
        kown = lax.dynamic_index_in_dim(kblk, cblk, axis=2, keepdims=False)
        vown = lax.dynamic_index_in_dim(vblk, cblk, axis=2, keepdims=False)
        s_own = jnp.einsum('bqkgd,bknd->bqkgn', qc, kown).astype(jnp.float32) * scale
        kpos = cblk * MOBA_BLOCK + jnp.arange(MOBA_BLOCK)
        causal = kpos[None, :] <= t[:, None]
        s_own = jnp.where(causal[None, :, None, None, :], s_own, NEG)
        nsel = kk * MOBA_BLOCK
        logits = jnp.concatenate([s_sel.reshape(B, MOBA_Q_CHUNK, kvh, G, nsel), s_own], axis=-1)
        p = jax.nn.softmax(logits, axis=-1).astype(v.dtype)
        o = jnp.einsum('bqkgm,bqkgmd->bqkgd', p[..., :nsel],
                       vsel.reshape(B, MOBA_Q_CHUNK, kvh, G, nsel, d))
        o = o + jnp.einsum('bqkgn,bknd->bqkgd', p[..., nsel:], vown)
        return o

    out = lax.map(chunk, (jnp.arange(nc), q_c, idx_c))
    return out.transpose(1, 0, 2, 3, 4, 5).reshape(B, S, H, d)


def setup_inputs(seed: int = 0) -> dict:
    key = jax.random.key(seed)
    ks = jax.random.split(key, 7)
    x = jax.random.normal(ks[0], (BATCH, SEQ, D_MODEL), jnp.float32)
    norm_g = 1.0 + 0.02 * jax.random.normal(ks[1], (DEPTH, D_MODEL), jnp.float32)
    w_in_a = jax.random.normal(ks[2], (N_LAYERS_A, D_MODEL, IN_A), jnp.float32) * D_MODEL ** -0.5
    sinks_a = jax.random.normal(ks[3], (N_LAYERS_A, N_HEADS), jnp.float32)
    w_in_b = jax.random.normal(ks[4], (N_LAYERS_B, D_MODEL, IN_B), jnp.float32) * D_MODEL ** -0.5
    w_out = jax.random.normal(ks[5], (DEPTH, MIX_WIDTH, D_MODEL), jnp.float32) * MIX_WIDTH ** -0.5
    final_g = 1.0 + 0.02 * jax.random.normal(ks[6], (D_MODEL,), jnp.float32)
    return {"x": x, "norm_g": norm_g, "w_in_a": w_in_a, "sinks_a": sinks_a,
            "w_in_b": w_in_b, "w_out": w_out, "final_g": final_g}


def reference(x, norm_g, w_in_a, sinks_a, w_in_b, w_out, final_g):
    B, S, _ = x.shape
    cos, sin = rope_tables(S)
    h = x
    for i in range(DEPTH):
        hn = rmsnorm(h, norm_g[i])
        j = i // N_MIXERS
        if i % N_MIXERS == 0:
            q, k, v, z = split_proj(hn, w_in_a[j], N_KV_A)
            q, k = apply_rope(q, cos, sin), apply_rope(k, cos, sin)
            o = sliding_window_sink_attention(q, k, v, sinks_a[j])
        else:
            q, k, v, z = split_proj(hn, w_in_b[j], N_KV_B)
            q, k = apply_rope(q, cos, sin), apply_rope(k, cos, sin)
            o = moba_attention(q, k, v)
        o = o.reshape(B, S, MIX_WIDTH) * jax.nn.silu(z)
        h = h + o @ w_out[i]
    return rmsnorm(h, final_g)
```

```python
import contextlib
import numpy as np
import ml_dtypes
import concourse.bass as bass
import concourse.mybir as mybir
from concourse.bass_utils import run_bass_kernel_spmd

F32 = mybir.dt.float32
BF16 = mybir.dt.bfloat16
ALU = mybir.AluOpType
AF = mybir.ActivationFunctionType
AX = mybir.AxisListType

S = 4096
D = 1024
NH = 16
HD = 64
NT = S // 128
NG = S // 512
IN_A = 2304
IN_B = 2560
EPS = 1e-5
BIG = 30000.0
DEBUG_NG = NG
ROPE_ADD_ENG = "pool"
STRICT_SAME_ENGINE = False

ENGS = ("pe", "act", "dve", "pool", "sp")


class Op:
    __slots__ = ("eng", "fn", "idx", "waits", "signal", "sigval", "dkey", "dval", "is_dma")

    def __init__(self, eng, fn):
        self.eng = eng
        self.fn = fn
        self.idx = -1
        self.waits = []
        self.signal = False
        self.sigval = 0
        self.dkey = None
        self.dval = 0
        self.is_dma = False


class Prog:
    def __init__(self, same_engine_raw=True):
        self.streams = {e: [] for e in ENGS}
        self.reg = {}
        self.waited = {e: {} for e in ENGS}
        self.dma_count = {}
        self.same_engine_raw = same_engine_raw

    def _add_dep(self, op, dep, kind):
        if dep is None or dep is op:
            return
        if dep.is_dma:
            w = self.waited[op.eng]
            if w.get(("d", dep.dkey), 0) >= dep.dval:
                return
            w[("d", dep.dkey)] = dep.dval
            op.waits = [x for x in op.waits if not (x.is_dma and x.dkey == dep.dkey)]
            op.waits.append(dep)
            return
        if dep.eng == op.eng and not op.is_dma:
            if op.eng == "pe":
                return
            if (kind != "raw" and not STRICT_SAME_ENGINE) or not self.same_engine_raw:
                return
        w = self.waited[op.eng]
        if w.get(dep.eng, -1) >= dep.idx:
            return
        w[dep.eng] = dep.idx
        dep.signal = True
        op.waits = [x for x in op.waits if x.is_dma or x.eng != dep.eng]
        op.waits.append(dep)

    def op(self, eng, fn, reads=(), writes=(), dma=None):
        writes = list(writes) + [k for k in reads if k.startswith("ps")]
        reads = [k for k in reads if not k.startswith("ps")]
        o = Op(eng, fn)
        o.idx = len(self.streams[eng])
        if dma is not None:
            o.is_dma = True
            o.dkey = dma
            self.dma_count[dma] = self.dma_count.get(dma, 0) + 1
            o.dval = 16 * self.dma_count[dma]
        for r in reads:
            st = self.reg.get(r)
            if st is not None:
                self._add_dep(o, st[0], "raw")
        for w in writes:
            st = self.reg.get(w)
            if st is not None:
                self._add_dep(o, st[0], "waw")
                for rd in st[1]:
                    self._add_dep(o, rd, "war")
        for w in writes:
            self.reg[w] = [o, []]
        for r in reads:
            st = self.reg.get(r)
            if st is None:
                self.reg[r] = [None, [o]]
            else:
                st[1].append(o)
        self.streams[eng].append(o)
        return o

    def emit(self, nc, final_wait_ops=()):
        for e in ENGS:
            c = 0
            for o in self.streams[e]:
                if o.is_dma:
                    continue
                if o.signal:
                    c += 1
                    o.sigval = c
        dkeys = sorted(self.dma_count.keys())
        with contextlib.ExitStack() as es:
            esem = {e: es.enter_context(nc.semaphore("sem_" + e)) for e in ENGS}
            dsem = {k: es.enter_context(nc.semaphore("dsem_" + k)) for k in dkeys}
            block = es.enter_context(nc.Block())
            streams = self.streams

            def run(e, eng):
                for o in streams[e]:
                    for d in o.waits:
                        if d.is_dma:
                            eng.wait_ge(dsem[d.dkey], d.dval)
                        else:
                            eng.wait_ge(esem[d.eng], d.sigval)
                    ins = o.fn(eng)
                    if o.is_dma:
                        ins.then_inc(dsem[o.dkey], 16)
                    elif o.signal:
                        ins.then_inc(esem[e], 1)
                if e == "sp":
                    done = {}
                    for d in final_wait_ops:
                        done[d.dkey] = max(done.get(d.dkey, 0), d.dval)
                    for k, v in done.items():
                        eng.wait_ge(dsem[k], v)

            @block.tensor
            def _(eng):
                run("pe", eng)

            @block.scalar
            def _(eng):
                run("act", eng)

            @block.vector
            def _(eng):
                run("dve", eng)

            @block.gpsimd
            def _(eng):
                run("pool", eng)

            @block.sync
            def _(eng):
                run("sp", eng)


def _host_consts():
    bf = ml_dtypes.bfloat16
    pos = np.arange(S, dtype=np.float32)
    inv = (10000.0 ** (-np.arange(0, HD, 2, dtype=np.float32) / HD)).astype(np.float32)
    ang = (pos[None, :] * inv[:, None]).astype(np.float32)
    c32, s32 = np.cos(ang).astype(np.float32), np.sin(ang).astype(np.float32)
    cosT = np.concatenate([c32, c32, c32, c32], 0)
    t2T = np.concatenate([-s32, s32, -s32, s32], 0)
    ident = np.eye(128, dtype=np.float32).astype(bf)
    kk = np.arange(128)[:, None]
    qq = np.arange(128)[None, :]
    cur = np.where(kk <= qq, 0.0, -BIG).astype(np.float32)
    prev = np.where(kk > qq, 0.0, -BIG).astype(np.float32)
    maskT = np.stack([np.tile(cur, (1, 4)), np.tile(prev, (1, 4))], 1).astype(bf)
    kaug = np.where((np.arange(S)[None, :] // 256) == np.arange(16)[:, None], BIG, 0.0).astype(np.float32).astype(bf)
    qb = np.arange(16)[:, None]
    jj = np.arange(16)[None, :]
    pm = np.where(jj < qb, 0.0, -1e30).astype(np.float32)
    own = (jj == qb).astype(np.float32)
    pmT = np.broadcast_to(pm[None], (128, 16, 16)).copy()
    ownT = np.broadcast_to(own[None], (128, 16, 16)).copy()
    return {"c_cos": cosT, "c_t2": t2T, "c_ident": ident, "c_mask": maskT.reshape(128, 1024),
            "c_kaug": kaug, "c_pm": pmT.reshape(128, 256), "c_own": ownT.reshape(128, 256)}


def build_program(layers, final_norm):
    nc = bass.Bass("TRN2", target_bir_lowering=False)
    dram = lambda n, s, dt, k: nc.dram_tensor(n, s, dt, kind=k).ap()
    x_d = dram("x", [S, D], F32, "ExternalInput")
    ng_d = dram("norm_g", [4, D], F32, "ExternalInput")
    wa_d = dram("w_in_a", [2, D, IN_A], F32, "ExternalInput")
    sk_d = dram("sinks_a", [2, NH], F32, "ExternalInput")
    wb_d = dram("w_in_b", [2, D, IN_B], F32, "ExternalInput")
    wo_d = dram("w_out", [4, D, D], F32, "ExternalInput")
    fg_d = dram("final_g", [D], F32, "ExternalInput")
    cos_d = dram("c_cos", [128, S], F32, "ExternalInput")
    t2_d = dram("c_t2", [128, S], F32, "ExternalInput")
    id_d = dram("c_ident", [128, 128], BF16, "ExternalInput")
    mk_d = dram("c_mask", [128, 1024], BF16, "ExternalInput")
    ka_d = dram("c_kaug", [16, S], BF16, "ExternalInput")
    pm_d = dram("c_pm", [128, 256], F32, "ExternalInput")
    ow_d = dram("c_own", [128, 256], F32, "ExternalInput")
    y_d = dram("y", [S, D], F32, "ExternalOutput")
    hs_d = y_d

    with contextlib.ExitStack() as es:
        sb = lambda n, s, dt: es.enter_context(nc.sbuf_tensor(n, s, dt))
        pp = lambda n, s, dt: es.enter_context(nc.psum_tensor(n, s, dt))
        win = sb("win", [128, 8, IN_B], BF16)
        wout = sb("wout", [128, 8, D], BF16)
        KT = sb("KT", [128, 4, S], BF16)
        Vcf = sb("Vcf", [128, NT * 4 * 65 + 64], BF16)
        Vc = Vcf[:, 0:NT * 4 * 65].rearrange("p (c k d) -> p c k d", c=NT, k=4)
        QT = sb("QT", [128, NH, 512], BF16)
        szT = sb("szT", [128, 8, 512], BF16)
        hnT = sb("hnT", [128, 8, 512], BF16)
        hl = [sb("hl%d" % i, [128, D], F32) for i in range(2)]
        hr = [sb("hr%d" % i, [128, D], F32) for i in range(2)]
        hn = [sb("hn%d" % i, [128, D], BF16) for i in range(2)]
        cosG = sb("cosG", [128, 512], F32)
        t2G = sb("t2G", [128, 512], F32)
        tAs = [sb("tA%d" % i, [128, 512], F32) for i in range(2)]
        tBs = [sb("tB%d" % i, [128, 512], F32) for i in range(2)]
        NPT = 4
        pT = [sb("pT%d" % i, [128, 512], BF16) for i in range(NPT)]
        maskT = sb("maskT", [128, 2, 512], BF16)
        ident = sb("ident", [128, 128], BF16)
        ones_bf = sb("ones_bf", [128, 128], BF16)
        gbc = sb("gbc", [128, D], F32)
        fgbc = sb("fgbc", [128, D], F32)
        szr = [sb("szr%d" % i, [128, 2, 128], BF16) for i in range(2)]
        gatedT = [sb("gatedT%d" % i, [128, 8, 128], BF16) for i in range(2)]
        gm = sb("gm", [128, NH, 16], F32)
        sel = sb("sel", [128, NH, 16], F32)
        m8 = sb("m8", [128, NH, 8], F32)
        biasb = sb("biasb", [128, NH, 16], BF16)
        pmT = sb("pmT", [128, 16, 16], F32)
        ownT = sb("ownT", [128, 16, 16], F32)
        kms = sb("kms", [64, 4, 2], F32)
        kmT = sb("kmT", [64, 4, 16], BF16)
        es_raw = sb("es_raw", [128, NH], F32)
        es_t = sb("es_t", [128, NH], F32)
        es_bf = sb("es_bf", [128, NH], BF16)
        e64 = sb("e64", [128, 128], BF16)
        ss = sb("ss", [128, 8], F32)
        nrm = sb("nrm", [128, 512], F32)
        rr = [sb("rr%d" % i, [128, 512], BF16) for i in range(4)]
        orow = sb("orow", [128, 128], BF16)
        NOSB = 4
        osb = [sb("osb%d" % i, [128, 512], F32) for i in range(NOSB)]

        psT = pp("psT", [128, 1024], BF16)
        psIP = [pp("psIP%d" % i, [128, 512], F32) for i in range(2)]
        psS = [pp("psS%d" % i, [128, 512], F32) for i in range(2)]
        psO = [pp("psO%d" % i, [128, 512], F32) for i in range(2)]
        psG = pp("psG", [128, 512], F32)

        P = Prog()
        op = P.op

        WIN_KEYS = ["win%d" % k for k in range(8)]
        WOUT_KEYS = ["wout%d" % k for k in range(8)]
        KTAUG = ["KTaug%d" % g for g in range(4)]
        op("sp", lambda e: e.dma_start(out=ident[:], in_=id_d[:, :]), writes=["ident"], dma="c0")
        op("sp", lambda e: e.dma_start(out=maskT[:].rearrange("p a b -> p (a b)"), in_=mk_d[:, :]), writes=["maskT"], dma="c1")
        op("sp", lambda e: e.dma_start(out=pmT[:].rearrange("p a b -> p (a b)"), in_=pm_d[:, :]), writes=["pmT"], dma="c2")
        op("sp", lambda e: e.dma_start(out=ownT[:].rearrange("p a b -> p (a b)"), in_=ow_d[:, :]), writes=["ownT"], dma="c3")
        op("sp", lambda e: e.dma_start(out=fgbc[:], in_=fg_d.partition_broadcast(128)), writes=["fgbc"], dma="c5")
        op("dve", lambda e: e.memset(ones_bf[:], 1.0), writes=["ones_bf"])
        op("dve", lambda e: e.memset(Vcf[:], 0.0), writes=["Vones"])
        op("dve", lambda e: e.memset(Vcf[:, 0:NT * 4 * 65].rearrange("p (c d) -> p c d", d=65)[:, :, 64:65], 1.0), reads=["Vones"], writes=["Vones"])
        op("dve", lambda e: e.memset(KT[64:128, :, :], 0.0), writes=KTAUG + ["KTpad"])
        op("dve", lambda e: e.memset(QT[64:128, :, :], 0.0), writes=["QTa%d" % i for i in range(4)])
        for g in range(4):
            op("sp", lambda e, g=g: e.dma_start(out=KT[64:80, g, :], in_=ka_d[:, :]), writes=["KTaug%d" % g], dma="c4_%d" % g)
        op("dve", lambda e: e.memset(kmT[:], 0.0), writes=["kmT"])
        op("dve", lambda e: e.memset(e64[:], 0.0), writes=["e64"])
        op("dve", lambda e: e.memset(es_bf[:], 0.0), writes=["es_bf"])
        op("dve", lambda e: e.memset(orow[:], 0.0), writes=["orow"])
        op("dve", lambda e: e.memset(orow[0:1, :], 1.0), reads=["orow"], writes=["orow"])
        for i_ in range(4):
            op("dve", lambda e, i_=i_: e.memset(rr[i_][:], 0.0), writes=["rr%d" % i_])
        op("dve", lambda e: e.memset(e64[0:1, 64:65], 1.0), reads=["e64"], writes=["e64"])

        out_ops = []

        win_loaded = set()
        stepN_prefetched = set()

        def load_win(li2):
            L2 = layers[li2]
            b2 = (L2 % 2 == 1)
            j2 = L2 // 2
            IN2 = IN_B if b2 else IN_A
            wd2 = wb_d if b2 else wa_d
            for kc in range(8):
                op("pool", lambda e, kc=kc: e.dma_start(out=win[:, kc, 0:IN2], in_=wd2[j2, kc * 128:(kc + 1) * 128, :]),
                   writes=[WIN_KEYS[kc]], dma="wi%d" % kc)
            win_loaded.add(li2)

        def emit_layer(li, L):
            is_b = (L % 2 == 1)
            j = L // 2
            nkv = 4 if is_b else 2
            IN = IN_B if is_b else IN_A
            kvw = nkv * HD
            koff, voff, zoff = 1024, 1024 + kvw, 1024 + 2 * kvw
            w_d = wb_d if is_b else wa_d
            KK = 128
            src_d = x_d if li == 0 else hs_d
            dst_d = y_d if li == len(layers) - 1 else hs_d
            last = (li == len(layers) - 1)
            do_fn = last and final_norm

            if li not in win_loaded:
                load_win(li)
            def load_wout():
                for kc in range(8):
                    op("pool", lambda e, kc=kc: e.dma_start(out=wout[:, kc, :], in_=wo_d[L, kc * 128:(kc + 1) * 128, :]),
                       writes=[WOUT_KEYS[kc]], dma="wo%d" % kc)
            if li != 0:
                load_wout()
            if li not in stepN_prefetched:
                op("sp", lambda e: e.dma_start(out=gbc[:], in_=ng_d[L].partition_broadcast(128)), writes=["gbc"], dma="g")
            if not is_b:
                op("dve", lambda e: e.memset(QT[64:128, :, :], 0.0), writes=["QTa%d" % i for i in range(4)])
                op("sp", lambda e: e.dma_start(out=es_raw[:], in_=sk_d[j].partition_broadcast(128)), writes=["es_raw"], dma="sk")
                op("act", lambda e: e.activation(out=es_t[:], in_=es_raw[:], func=AF.Exp), reads=["es_raw"], writes=["es_t"])
                op("dve", lambda e: e.tensor_copy(out=es_bf[0:1, :], in_=es_t[0:1, :]), reads=["es_t"], writes=["es_bf"])

            def rms_stats(src, src_key, col):
                c = ss[:, col:col + 1]
                if is_b:
                    op("dve", lambda e: e.scalar_tensor_tensor(out=tAs[0][:].bitcast(BF16), in0=src, scalar=1.0, in1=src, op0=ALU.mult, op1=ALU.mult, accum_out=c),
                       reads=[src_key], writes=["tA0", "ss%d" % col])
                else:
                    op("act", lambda e: e.activation(out=tAs[0][:].bitcast(BF16), in_=src, func=AF.Square, accum_out=c),
                       reads=[src_key], writes=["tA0", "ss%d" % col])
                op("dve", lambda e: e.tensor_scalar(out=c, in0=c, scalar1=1.0 / D, scalar2=EPS, op0=ALU.mult, op1=ALU.add),
                   reads=["ss%d" % col], writes=["ss%d" % col])
                op("act", lambda e: e.activation(out=c, in_=c, func=AF.Ln), reads=["ss%d" % col], writes=["ss%d" % col])
                op("act", lambda e: e.activation(out=c, in_=c, func=AF.Exp, scale=-0.5), reads=["ss%d" % col], writes=["ss%d" % col])
                return c

            ipb = [0]

            def next_ip():
                ipb[0] ^= 1
                return ipb[0]

            rope_cnt = [0]

            def rope_tile(pk, ps, dst0, dst1, dk0, dk1):
                ri = rope_cnt[0] % 2
                rope_cnt[0] += 1
                tA, tB = tAs[ri], tBs[ri]
                ka = "tA%d" % ri
                kb_ = "tB%d_" % ri
                op("dve", lambda e: e.tensor_tensor(out=tA[:], in0=ps[:], in1=cosG[:], op=ALU.mult), reads=[pk, "cosG"], writes=[ka])
                for (o0, i0) in ((0, 32), (32, 0), (64, 96), (96, 64)):
                    op("dve", lambda e, o0=o0, i0=i0: e.tensor_tensor(out=tB[o0:o0 + 32, :], in0=ps[i0:i0 + 32, :], in1=t2G[o0:o0 + 32, :], op=ALU.mult),
                       reads=[pk, "t2G"], writes=[kb_ + str(o0)])
                op(ROPE_ADD_ENG, lambda e: e.tensor_tensor(out=dst0, in0=tA[0:64, :], in1=tB[0:64, :], op=ALU.add),
                   reads=[ka, kb_ + "0", kb_ + "32"], writes=[dk0])
                op(ROPE_ADD_ENG, lambda e: e.tensor_tensor(out=dst1, in0=tA[64:128, :], in1=tB[64:128, :], op=ALU.add),
                   reads=[ka, kb_ + "64", kb_ + "96"], writes=[dk1])

            def emit_stepN_dma(G, tt, src=None):
                t = 4 * G + tt
                sl = t % 2
                src = src_d if src is None else src
                op("sp", lambda e: e.dma_start(out=hl[sl][:], in_=src[t * 128:(t + 1) * 128, :]),
                   reads=["hd%d" % t], writes=["hl%d" % sl], dma="hl%d" % sl)

            def emit_stepN_a1(G, tt):
                t = 4 * G + tt
                sl = t % 2
                c = ss[:, 4 + tt:5 + tt]
                sk = "ssN%d" % tt
                src = hl[sl][:]
                if is_b:
                    op("dve", lambda e: e.scalar_tensor_tensor(out=tAs[0][:].bitcast(BF16), in0=src, scalar=1.0, in1=src, op0=ALU.mult, op1=ALU.mult, accum_out=c),
                       reads=["hl%d" % sl], writes=["tA0", sk])
                else:
                    op("act", lambda e: e.activation(out=tAs[0][:].bitcast(BF16), in_=src, func=AF.Square, accum_out=c),
                       reads=["hl%d" % sl], writes=["tA0", sk])
                op("dve", lambda e: e.tensor_scalar(out=c, in0=c, scalar1=1.0 / D, scalar2=EPS, op0=ALU.mult, op1=ALU.add),
                   reads=[sk], writes=[sk])

            def emit_stepN_a2(G, tt):
                c = ss[:, 4 + tt:5 + tt]
                sk = "ssN%d" % tt
                op("act", lambda e: e.activation(out=c, in_=c, func=AF.Ln), reads=[sk], writes=[sk])
                op("act", lambda e: e.activation(out=c, in_=c, func=AF.Exp, scale=-0.5), reads=[sk], writes=[sk])

            def emit_stepN_a3(G, tt):
                t = 4 * G + tt
                sl = t % 2
                c = ss[:, 4 + tt:5 + tt]
                op("dve", lambda e: e.scalar_tensor_tensor(out=hn[sl][:], in0=hl[sl][:], scalar=c, in1=gbc[:], op0=ALU.mult, op1=ALU.mult),
                   reads=["hl%d" % sl, "ssN%d" % tt, "gbc"], writes=["hn%d" % sl])

            def emit_stepN_a(G, tt):
                emit_stepN_dma(G, tt)
                emit_stepN_a1(G, tt)
                emit_stepN_a2(G, tt)
                emit_stepN_a3(G, tt)

            def emit_stepN_b(G, tt):
                t = 4 * G + tt
                sl = t % 2
                for kc in range(8):
                    op("pe", lambda e, kc=kc: e.transpose(out=psT[:, kc * 128:(kc + 1) * 128], in_=hn[sl][:, kc * 128:(kc + 1) * 128], identity=ident[:]),
                       reads=["hn%d" % sl, "ident"], writes=["psT"])
                if is_b:
                    op("dve", lambda e: e.tensor_copy(out=hnT[:, :, tt * 128:(tt + 1) * 128], in_=psT[:].rearrange("p (k t) -> p k t", k=8)),
                       reads=["psT"], writes=["hnT%d" % tt])
                else:
                    op("act", lambda e: e.activation(out=hnT[:, :, tt * 128:(tt + 1) * 128], in_=psT[:].rearrange("p (k t) -> p k t", k=8), func=AF.Copy),
                       reads=["psT"], writes=["hnT%d" % tt])

            def emit_stepN_tile(G, tt):
                emit_stepN_a(G, tt)
                emit_stepN_b(G, tt)

            def emit_group(G):
                tok0 = 512 * G
                HNT = ["hnT%d" % tt for tt in range(4)]
                op("sp", lambda e, tok0=tok0: e.dma_start(out=cosG[:], in_=cos_d[:, tok0:tok0 + 512]), writes=["cosG"], dma="cos")
                op("sp", lambda e, tok0=tok0: e.dma_start(out=t2G[:], in_=t2_d[:, tok0:tok0 + 512]), writes=["t2G"], dma="t2")

                IPB = {"psIP0": psIP[0], "psIP1": psIP[1], "psS0": psS[0], "psS1": psS[1]}
                rb = [0]
                zb = [0]

                def proj_fm(col0, bk):
                    for kc in range(8):
                        op("pe", lambda e, kc=kc: e.matmul(IPB[bk][:], lhsT=win[:, kc, col0:col0 + 128], rhs=hnT[:, kc, :], start=(kc == 0), stop=(kc == 7)),
                           reads=[WIN_KEYS[kc]] + HNT, writes=[bk])

                def rope_bank():
                    rb[0] ^= 1
                    return "psIP%d" % rb[0]

                def z_bank():
                    zb[0] ^= 1
                    return "psS%d" % zb[0]

                for f in range(8):
                    bk = rope_bank()
                    proj_fm(f * 128, bk)
                    rope_tile(bk, IPB[bk], QT[0:64, 2 * f, :], QT[0:64, 2 * f + 1, :], "QT%d" % (2 * f), "QT%d" % (2 * f + 1))
                    if f == 0 and tail_box[0] is not None:
                        tail_box[0]()
                        tail_box[0] = None
                    bz = z_bank()
                    proj_fm(zoff + f * 128, bz)
                    op("act", lambda e, bz=bz, f=f: e.activation(out=szT[:, f, :], in_=IPB[bz][:], func=AF.Silu),
                       reads=[bz], writes=["szT%d" % f])
                    if f < nkv // 2:
                        kt = f
                        bk = rope_bank()
                        proj_fm(koff + kt * 128, bk)
                        rope_tile(bk, IPB[bk], KT[0:64, 2 * kt, tok0:tok0 + 512], KT[0:64, 2 * kt + 1, tok0:tok0 + 512],
                                  "KT%d_%d" % (2 * kt, G), "KT%d_%d" % (2 * kt + 1, G))
                    if f < 4:
                        tt = f
                        t = 4 * G + tt
                        bz = z_bank()
                        for kc in range(8):
                            op("pe", lambda e, bz=bz, kc=kc, tt=tt: e.matmul(IPB[bz][:, 0:kvw], lhsT=hnT[:, kc, tt * 128:(tt + 1) * 128], rhs=win[:, kc, voff:voff + kvw], start=(kc == 0), stop=(kc == 7)),
                               reads=[WIN_KEYS[kc]] + HNT, writes=[bz])
                        op("dve", lambda e, bz=bz, t=t: e.tensor_copy(out=Vc[:, t, 0:nkv, 0:64], in_=IPB[bz][:, 0:kvw].rearrange("p (k d) -> p k d", k=nkv)),
                           reads=[bz], writes=["V%d" % t])
                def stepG_a0(tt):
                    t = 4 * G + tt
                    qb = t // 2
                    qs = slice(tt * 128, (tt + 1) * 128)
                    QTK = ["QT%d" % h for h in range(NH)]
                    for h in range(NH):
                        op("pe", lambda e, h=h: e.matmul(psG[:, h * 16:(h + 1) * 16], lhsT=QT[0:64, h, qs], rhs=kmT[:, h // 4, :], start=True, stop=True),
                           reads=QTK + ["kmT"], writes=["psG"])
                    op("dve", lambda e: e.tensor_tensor(out=gm[:], in0=psG[:, 0:256].rearrange("p (h j) -> p h j", h=NH),
                                                        in1=pmT[:, qb:qb + 1, :].to_broadcast([128, NH, 16]), op=ALU.add),
                       reads=["psG", "pmT"], writes=["gm"])

                def stepG_max(tt, h0):
                    for h in range(h0, h0 + 4):
                        op("dve", lambda e, h=h: e.max(out=m8[:, h, :], in_=gm[:, h, :]), reads=["gm"], writes=["m8_%d" % h])

                def stepG_sel(tt):
                    op("dve", lambda e: e.tensor_tensor(out=sel[:], in0=gm[:], in1=m8[:, :, 2:3].to_broadcast([128, NH, 16]), op=ALU.is_ge),
                       reads=["gm"] + ["m8_%d" % h for h in range(NH)], writes=["sel"])

                def stepG_bias(tt):
                    qb = (4 * G + tt) // 2
                    op("dve", lambda e: e.scalar_tensor_tensor(out=biasb[:], in0=sel[:], scalar=-1.0, in1=ownT[:, qb:qb + 1, :].to_broadcast([128, NH, 16]),
                                                               op0=ALU.add, op1=ALU.add),
                       reads=["sel", "ownT"], writes=["biasb"])

                def stepG_b(tt, half):
                    qs = slice(tt * 128, (tt + 1) * 128)
                    for hh in range(8):
                        op("pe", lambda e, hh=hh: e.transpose(out=psT[0:16, hh * 128:(hh + 1) * 128], in_=biasb[:, half * 8 + hh, :], identity=ident[:]),
                           reads=["biasb", "ident"], writes=["psT"])
                    op("dve", lambda e: e.tensor_copy(out=QT[64:80, half * 8:(half + 1) * 8, qs], in_=psT[0:16, :].rearrange("p (h q) -> p h q", h=8)),
                       reads=["psT"], writes=["QTa%d" % tt])

                def emit_stepG_a(tt):
                    qb = (4 * G + tt) // 2
                    qs = slice(tt * 128, (tt + 1) * 128)
                    if qb < 4:
                        op("dve", lambda e: e.memset(QT[64:80, :, qs], 0.0), writes=["QTa%d" % tt])
                        return False
                    stepG_a0(tt)
                    for h0 in range(0, NH, 4):
                        stepG_max(tt, h0)
                    stepG_sel(tt)
                    stepG_bias(tt)
                    return True

                def emit_stepG_b(tt):
                    stepG_b(tt, 0)
                    stepG_b(tt, 1)

                def emit_stepG_tile(tt):
                    if emit_stepG_a(tt):
                        emit_stepG_b(tt)

                if li == 0 and G == 0:
                    load_wout()
                if G == DEBUG_NG - 1 and li + 1 < len(layers):
                    load_win(li + 1)
                if is_b:
                    KTG = ["KT%d_%d" % (g, G) for g in range(4)]
                    op("dve", lambda e, tok0=tok0: e.tensor_reduce(out=kms[:], in_=KT[0:64, :, tok0:tok0 + 512].rearrange("p k (b s) -> p k b s", b=2), axis=AX.X, op=ALU.add),
                       reads=KTG, writes=["kms"])
                    op("dve", lambda e, G=G: e.tensor_scalar(out=kmT[:, :, 2 * G:2 * G + 2], in0=kms[:], scalar1=1.0 / 256, scalar2=None, op0=ALU.mult),
                       reads=["kms"], writes=["kmT"])
                    emit_stepG_tile(0)
                    if 4 * G // 2 < 4:
                        for tt_ in range(1, 4):
                            emit_stepG_tile(tt_)
                units = []
                for tt in range(4):
                    t = 4 * G + tt
                    if is_b:
                        chunks = [(c, None) for c in range(t)] + [(t, 0)]
                    else:
                        chunks = ([(t - 1, 1)] if t >= 1 else []) + [(t, 0)]
                    for g4 in range(4):
                        for i, (c, mk) in enumerate(chunks):
                            units.append((tt, g4, i, c, mk, i == len(chunks) - 1))
                nu = len(units)
                SB = [psS[0], psS[1], psIP[1]]
                SBK = ["psS0", "psS1", "psIP1"]
                LA = 2
                pending = []

                def defer(due, chain, stage, fn):
                    pending.append((due, chain, stage, fn))

                def run_pending(u, max_stage=10, min_stage=0, max_chain=1 << 30):
                    while True:
                        cand = [p for p in pending if p[0] <= u and min_stage <= p[2] <= max_stage and p[1] <= max_chain]
                        if not cand:
                            return
                        cand.sort(key=lambda p: (p[1], p[2]))
                        pending.remove(cand[0])
                        cand[0][3]()

                DL1, DL2, DL3 = 3, 5, 8
                first_unit = {}
                for ui, un in enumerate(units):
                    first_unit.setdefault(un[0], ui)
                nxt_layer = (G + 1 == DEBUG_NG) and (li + 1 < len(layers))
                if nxt_layer:
                    op("sp", lambda e: e.dma_start(out=gbc[:], in_=ng_d[layers[li + 1]].partition_broadcast(128)), writes=["gbc"], dma="g")
                    stepN_prefetched.add(li + 1)
                if G + 1 < DEBUG_NG or nxt_layer:
                    Gn = (G + 1) if not nxt_layer else 0
                    srcn = None if not nxt_layer else hs_d
                    for tt_ in range(4):
                        if tt_ == 0:
                            ddue = 0
                        elif tt_ == 1:
                            ddue = 1
                        else:
                            ddue = max(first_unit[tt_ - 1] + 1, first_unit[tt_ - 2] + 7)
                        defer(ddue, tt_ * 4 - 3.9, 0, lambda tt_=tt_: emit_stepN_dma(Gn, tt_, srcn))
                        defer(first_unit[tt_] + 1, tt_ * 4 + 0.25, 0, lambda tt_=tt_: emit_stepN_a1(Gn, tt_))
                        defer(first_unit[tt_] + 4, tt_ * 4 + 0.26, 0, lambda tt_=tt_: emit_stepN_a2(Gn, tt_))
                        defer(first_unit[tt_] + 6, tt_ * 4 + 0.27, 0, lambda tt_=tt_: emit_stepN_a3(Gn, tt_))
                        defer(first_unit[tt_] + 10, tt_ * 4 + 0.75, 0, lambda tt_=tt_: emit_stepN_b(Gn, tt_))
                if is_b and (4 * G) // 2 >= 4:
                    for tt_ in range(3):
                        fu = first_unit[tt_]
                        defer(fu + 3, tt_ * 4 + 0.50, 0, lambda tt_=tt_: stepG_a0(tt_ + 1))
                        for i_ in range(4):
                            defer(fu + 5 + i_, tt_ * 4 + 0.51 + 0.01 * i_, 0, lambda tt_=tt_, i_=i_: stepG_max(tt_ + 1, 4 * i_))
                        defer(fu + 9, tt_ * 4 + 0.56, 0, lambda tt_=tt_: stepG_sel(tt_ + 1))
                        defer(fu + 10, tt_ * 4 + 0.57, 0, lambda tt_=tt_: stepG_bias(tt_ + 1))
                        defer(fu + 14, tt_ * 4 + 0.60, 0, lambda tt_=tt_: stepG_b(tt_ + 1, 0))
                        defer(fu + 17, tt_ * 4 + 0.61, 0, lambda tt_=tt_: stepG_b(tt_ + 1, 1))

                def emit_qk(u):
                    tt, g4, i, c, mk, lastc = units[u]
                    kvh = g4 if is_b else g4 // 2
                    sbk = u % 3
                    qs = slice(tt * 128, (tt + 1) * 128)
                    rd = ["KT%d_%d" % (kvh, c // 4)] + ["QT%d" % h for h in range(4 * g4, 4 * g4 + 4)]
                    rd += ["QTa%d" % tt] + KTAUG
                    op("pe", lambda e: e.matmul(SB[sbk][:], lhsT=KT[0:KK, kvh, c * 128:(c + 1) * 128], rhs=QT[0:KK, 4 * g4:4 * g4 + 4, qs], start=True, stop=(mk is None)),
                       reads=rd, writes=[SBK[sbk]])
                    if mk is not None:
                        op("pe", lambda e: e.matmul(SB[sbk][:], lhsT=ident[:], rhs=maskT[:, mk, :], start=False, stop=True),
                           reads=["ident", "maskT"], writes=[SBK[sbk]])

                def emit_exp_pv(u):
                    tt, g4, i, c, mk, lastc = units[u]
                    kvh = g4 if is_b else g4 // 2
                    sbk = u % 3
                    ps_ = u % NPT
                    ob = (tt * 4 + g4) % 2
                    op("act", lambda e: e.activation(out=pT[ps_][:], in_=SB[sbk][:], func=AF.Exp, scale=0.125),
                       reads=[SBK[sbk]], writes=["pT%d" % ps_])
                    op("pe", lambda e: e.matmul(psO[ob][:, :], lhsT=Vcf[:, (c * 4 + kvh) * 65:(c * 4 + kvh) * 65 + 128], rhs=pT[ps_][:], start=(i == 0), stop=(lastc and is_b)),
                       reads=["V%d" % c, "Vones", "pT%d" % ps_], writes=["psO%d" % ob])
                    if lastc and not is_b:
                        op("pe", lambda e: e.matmul(psO[ob][:, :], lhsT=e64[:, :], rhs=es_bf[:, 4 * g4:4 * g4 + 4].unsqueeze(2).to_broadcast([128, 4, 128]), start=False, stop=True),
                           reads=["e64", "es_bf"], writes=["psO%d" % ob])
                    if lastc:
                        ok = "psO%d" % ob
                        chain = tt * 4 + g4
                        kb = chain % NOSB
                        run_pending(1 << 30, max_stage=2, min_stage=1, max_chain=chain - NOSB)
                        op("dve", lambda e: e.tensor_copy(out=osb[kb][0:65, :], in_=psO[ob][0:65, :]), reads=[ok], writes=["osb%d" % kb])
                        defer(u + DL1, chain, 1, lambda: norm_stage1(u, tt, g4))

                def norm_stage1(u0, tt, g4):
                    kb = (tt * 4 + g4) % NOSB
                    op("act", lambda e: e.activation(out=nrm[32:33, :], in_=osb[kb][64:65, :], func=AF.Ln), reads=["osb%d" % kb], writes=["nrm_l"])
                    op("act", lambda e: e.activation(out=rr[kb][0:1, :], in_=nrm[32:33, :], func=AF.Exp, scale=-1.0), reads=["nrm_l"], writes=["rr%d" % kb])
                    defer(u0 + DL2, tt * 4 + g4, 2, lambda: norm_stage2(u0, tt, g4))

                def norm_stage2(u0, tt, g4):
                    qs = slice(tt * 128, (tt + 1) * 128)
                    gi = (4 * G + tt) % 2
                    kb = (tt * 4 + g4) % NOSB
                    op("pe", lambda e: e.matmul(psG[:], lhsT=orow[:, :], rhs=rr[kb][:, :], start=True, stop=True),
                       reads=["orow", "rr%d" % kb], writes=["psG"])
                    for par in range(2):
                        p0 = 64 * par
                        sz_in = szT[p0:p0 + 64, 2 * g4:2 * g4 + 2, qs]
                        rb_in = psG[p0:p0 + 64, :].rearrange("p (h q) -> p h q", h=4)[:, par::2, :]
                        o_in = osb[kb][0:64, :].rearrange("p (h q) -> p h q", h=4)[:, par::2, :]
                        op("dve", lambda e, sz_in=sz_in, rb_in=rb_in, par=par: e.tensor_tensor(out=szr[par][0:64, :, :], in0=sz_in, in1=rb_in, op=ALU.mult),
                           reads=["szT%d" % (2 * g4), "szT%d" % (2 * g4 + 1), "psG"], writes=["szr%d" % par])
                        op("dve", lambda e, o_in=o_in, par=par, p0=p0: e.tensor_tensor(out=gatedT[gi][p0:p0 + 64, 2 * g4:2 * g4 + 2, :], in0=o_in, in1=szr[par][0:64, :, :], op=ALU.mult),
                           reads=["osb%d" % kb, "szr%d" % par], writes=["gT%d_%d_%d" % (gi, g4, par)])
                    if g4 == 3:
                        defer(u0 + DL3, tt * 4 + g4, 3, lambda: emit_outproj(tt, u0 + DL3))

                def emit_outproj(tt, ub):
                    t = 4 * G + tt
                    gi = t % 2
                    sl = t % 2
                    chain = tt * 4 + 3
                    GT = ["gT%d_%d_%d" % (gi, g4, par) for g4 in range(4) for par in range(2)]
                    op("sp", lambda e: e.dma_start(out=hr[sl][:], in_=src_d[t * 128:(t + 1) * 128, :]),
                       reads=["hd%d" % t], writes=["hr%d" % sl], dma="hr%d" % sl)

                    def piece(n, f0):
                        for f in (f0, f0 + 1):
                            op("pe", lambda e, f=f: e.matmul(psIP[0][:], lhsT=gatedT[gi][:, f, :], rhs=wout[:, f, n * 512:(n + 1) * 512], start=(f == 0), stop=(f == 7)),
                               reads=GT + WOUT_KEYS, writes=["psIP0"])

                    def add_half(n):
                        op("dve", lambda e: e.tensor_tensor(out=hr[sl][:, n * 512:(n + 1) * 512], in0=psIP[0][:], in1=hr[sl][:, n * 512:(n + 1) * 512], op=ALU.add),
                           reads=["psIP0", "hr%d" % sl], writes=["hr%d" % sl])
                        if n == 1:
                            finish()

                    def finish():
                        if do_fn:
                            c = rms_stats(hr[sl][:], "hr%d" % sl, 1)
                            op("dve", lambda e, c=c: e.scalar_tensor_tensor(out=hr[sl][:], in0=hr[sl][:], scalar=c, in1=fgbc[:], op0=ALU.mult, op1=ALU.mult),
                               reads=["hr%d" % sl, "ss1", "fgbc"], writes=["hr%d" % sl])
                        o = op("pool", lambda e: e.dma_start(out=dst_d[t * 128:(t + 1) * 128, :], in_=hr[sl][:]),
                               reads=["hr%d" % sl], writes=["hd%d" % t], dma="st%d" % sl)
                        if last:
                            out_ops.append(o)

                    n_next = sum(1 for un in units if un[0] == tt + 1)
                    if n_next >= 14:
                        for n in range(2):
                            base = ub + n * 6
                            for k in range(4):
                                defer(base + k, chain, 3.0 + 0.01 * (n * 6 + k + 1), lambda n=n, k=k: piece(n, 2 * k))
                            defer(base + 5, chain, 3.0 + 0.01 * (n * 6 + 6), lambda n=n: add_half(n))
                    else:
                        OPB = [(psIP[0][:], "psIP0"), (psT[:].bitcast(F32), "psT")]
                        for n in range(2):
                            for f in range(8):
                                op("pe", lambda e, n=n, f=f: e.matmul(OPB[n][0], lhsT=gatedT[gi][:, f, :], rhs=wout[:, f, n * 512:(n + 1) * 512], start=(f == 0), stop=(f == 7)),
                                   reads=GT + WOUT_KEYS, writes=[OPB[n][1]])
                        for n in range(2):
                            op("dve", lambda e, n=n: e.tensor_tensor(out=hr[sl][:, n * 512:(n + 1) * 512], in0=OPB[n][0], in1=hr[sl][:, n * 512:(n + 1) * 512], op=ALU.add),
                               reads=[OPB[n][1], "hr%d" % sl], writes=["hr%d" % sl])
                        finish()

                for v in range(min(LA, nu)):
                    emit_qk(v)
                for u in range(nu):
                    if u + LA < nu:
                        emit_qk(u + LA)
                    emit_exp_pv(u)
                    run_pending(u)
                run_pending(1 << 30, max_stage=0.9)

                def flush_tail():
                    while pending:
                        run_pending(1 << 30)
                if G + 1 < DEBUG_NG:
                    tail_box[0] = flush_tail
                else:
                    flush_tail()

            tail_box = [None]
            if li not in stepN_prefetched:
                for tt in range(4):
                    emit_stepN_tile(0, tt)
            for G in range(DEBUG_NG):
                emit_group(G)

        for li, L in enumerate(layers):
            emit_layer(li, L)

        P.emit(nc, final_wait_ops=out_ops)
    return nc


_CONSTS = None
_PROGS = {}


def _get_prog(layers, final_norm):
    key = (tuple(layers), final_norm)
    if key not in _PROGS:
        _PROGS[key] = build_program(list(layers), final_norm)
    return _PROGS[key]


LAUNCH_PLAN = [([0, 1, 2, 3], True)]


def kernel(x, norm_g, w_in_a, sinks_a, w_in_b, w_out, final_g):
    global _CONSTS
    if _CONSTS is None:
        _CONSTS = _host_consts()
    n = 8
    x = np.ascontiguousarray(np.asarray(x, dtype=np.float32))
    shared = {"norm_g": np.ascontiguousarray(np.asarray(norm_g, np.float32)),
              "w_in_a": np.ascontiguousarray(np.asarray(w_in_a, np.float32)),
              "sinks_a": np.ascontiguousarray(np.asarray(sinks_a, np.float32)),
              "w_in_b": np.ascontiguousarray(np.asarray(w_in_b, np.float32)),
              "w_out": np.ascontiguousarray(np.asarray(w_out, np.float32)),
              "final_g": np.ascontiguousarray(np.asarray(final_g, np.float32))}
    shared.update(_CONSTS)
    cur = [x[b] for b in range(n)]
    for layers, fn in LAUNCH_PLAN:
        nc = _get_prog(layers, fn)
        in_maps = [dict(shared, x=cur[b]) for b in range(n)]
        res = run_bass_kernel_spmd(nc, in_maps, core_ids=list(range(n)))
        cur = [np.asarray(r["y"]) for r in res.results]
    return np.stack(cur, 0).astype(np.float32)
```

```python
import contextlib
import numpy as np
import ml_dtypes
import concourse.bass as bass
import concourse.mybir as mybir
from concourse.bass_utils import run_bass_kernel_spmd

F32 = mybir.dt.float32
BF16 = mybir.dt.bfloat16
ALU = mybir.AluOpType
AF = mybir.ActivationFunctionType
AX = mybir.AxisListType

S = 4096
D = 1024
NH = 16
HD = 64
NT = S // 128
NG = S // 512
IN_A = 2304
IN_B = 2560
EPS = 1e-5
BIG = 30000.0
DEBUG_NG = NG
ROPE_ADD_ENG = "pool"
STRICT_SAME_ENGINE = False

ENGS = ("pe", "act", "dve", "pool", "sp")


class Op:
    __slots__ = ("eng", "fn", "idx", "waits", "signal", "sigval", "dkey", "dval", "is_dma")

    def __init__(self, eng, fn):
        self.eng = eng
        self.fn = fn
        self.idx = -1
        self.waits = []
        self.signal = False
        self.sigval = 0
        self.dkey = None
        self.dval = 0
        self.is_dma = False


class Prog:
    def __init__(self, same_engine_raw=True):
        self.streams = {e: [] for e in ENGS}
        self.reg = {}
        self.waited = {e: {} for e in ENGS}
        self.dma_count = {}
        self.same_engine_raw = same_engine_raw

    def _add_dep(self, op, dep, kind):
        if dep is None or dep is op:
            return
        if dep.is_dma:
            w = self.waited[op.eng]
            if w.get(("d", dep.dkey), 0) >= dep.dval:
                return
            w[("d", dep.dkey)] = dep.dval
            op.waits = [x for x in op.waits if not (x.is_dma and x.dkey == dep.dkey)]
            op.waits.append(dep)
            return
        if dep.eng == op.eng and not op.is_dma:
            if op.eng == "pe":
                return
            if (kind != "raw" and not STRICT_SAME_ENGINE) or not self.same_engine_raw:
                return
        w = self.waited[op.eng]
        if w.get(dep.eng, -1) >= dep.idx:
            return
        w[dep.eng] = dep.idx
        dep.signal = True
        op.waits = [x for x in op.waits if x.is_dma or x.eng != dep.eng]
        op.waits.append(dep)

    def op(self, eng, fn, reads=(), writes=(), dma=None):
        writes = list(writes) + [k for k in reads if k.startswith("ps")]
        reads = [k for k in reads if not k.startswith("ps")]
        o = Op(eng, fn)
        o.idx = len(self.streams[eng])
        if dma is not None:
            o.is_dma = True
            o.dkey = dma
            self.dma_count[dma] = self.dma_count.get(dma, 0) + 1
            o.dval = 16 * self.dma_count[dma]
        for r in reads:
            st = self.reg.get(r)
            if st is not None:
                self._add_dep(o, st[0], "raw")
        for w in writes:
            st = self.reg.get(w)
            if st is not None:
                self._add_dep(o, st[0], "waw")
                for rd in st[1]:
                    self._add_dep(o, rd, "war")
        for w in writes:
            self.reg[w] = [o, []]
        for r in reads:
            st = self.reg.get(r)
            if st is None:
                self.reg[r] = [None, [o]]
            else:
                st[1].append(o)
        self.streams[eng].append(o)
        return o

    def emit(self, nc, final_wait_ops=()):
        for e in ENGS:
            c = 0
            for o in self.streams[e]:
                if o.is_dma:
                    continue
                if o.signal:
                    c += 1
                    o.sigval = c
        dkeys = sorted(self.dma_count.keys())
        with contextlib.ExitStack() as es:
            esem = {e: es.enter_context(nc.semaphore("sem_" + e)) for e in ENGS}
            dsem = {k: es.enter_context(nc.semaphore("dsem_" + k)) for k in dkeys}
            block = es.enter_context(nc.Block())
            streams = self.streams

            def run(e, eng):
                for o in streams[e]:
                    for d in o.waits:
                        if d.is_dma:
                            eng.wait_ge(dsem[d.dkey], d.dval)
                        else:
                            eng.wait_ge(esem[d.eng], d.sigval)
                    ins = o.fn(eng)
                    if o.is_dma:
                        ins.then_inc(dsem[o.dkey], 16)
                    elif o.signal:
                        ins.then_inc(esem[e], 1)
                if e == "sp":
                    done = {}
                    for d in final_wait_ops:
                        done[d.dkey] = max(done.get(d.dkey, 0), d.dval)
                    for k, v in done.items():
                        eng.wait_ge(dsem[k], v)

            @block.tensor
            def _(eng):
                run("pe", eng)

            @block.scalar
            def _(eng):
                run("act", eng)

            @block.vector
            def _(eng):
                run("dve", eng)

            @block.gpsimd
            def _(eng):
                run("pool", eng)

            @block.sync
            def _(eng):
                run("sp", eng)


def _host_consts():
    bf = ml_dtypes.bfloat16
    pos = np.arange(S, dtype=np.float32)
    inv = (10000.0 ** (-np.arange(0, HD, 2, dtype=np.float32) / HD)).astype(np.float32)
    ang = (pos[None, :] * inv[:, None]).astype(np.float32)
    c32, s32 = np.cos(ang).astype(np.float32), np.sin(ang).astype(np.float32)
    cosT = np.concatenate([c32, c32, c32, c32], 0)
    t2T = np.concatenate([-s32, s32, -s32, s32], 0)
    ident = np.eye(128, dtype=np.float32).astype(bf)
    kk = np.arange(128)[:, None]
    qq = np.arange(128)[None, :]
    cur = np.where(kk <= qq, 0.0, -BIG).astype(np.float32)
    prev = np.where(kk > qq, 0.0, -BIG).astype(np.float32)
    maskT = np.stack([np.tile(cur, (1, 4)), np.tile(prev, (1, 4))], 1).astype(bf)
    kaug = np.where((np.arange(S)[None, :] // 256) == np.arange(16)[:, None], BIG, 0.0).astype(np.float32).astype(bf)
    qb = np.arange(16)[:, None]
    jj = np.arange(16)[None, :]
    pm = np.where(jj < qb, 0.0, -1e30).astype(np.float32)
    own = (jj == qb).astype(np.float32)
    pmT = np.broadcast_to(pm[None], (128, 16, 16)).copy()
    ownT = np.broadcast_to(own[None], (128, 16, 16)).copy()
    return {"c_cos": cosT, "c_t2": t2T, "c_ident": ident, "c_mask": maskT.reshape(128, 1024),
            "c_kaug": kaug, "c_pm": pmT.reshape(128, 256), "c_own": ownT.reshape(128, 256)}


def build_program(layers, final_norm):
    nc = bass.Bass("TRN2", target_bir_lowering=False)
    dram = lambda n, s, dt, k: nc.dram_tensor(n, s, dt, kind=k).ap()
    x_d = dram("x", [S, D], F32, "ExternalInput")
    ng_d = dram("norm_g", [4, D], F32, "ExternalInput")
    wa_d = dram("w_in_a", [2, D, IN_A], F32, "ExternalInput")
    sk_d = dram("sinks_a", [2, NH], F32, "ExternalInput")
    wb_d = dram("w_in_b", [2, D, IN_B], F32, "ExternalInput")
    wo_d = dram("w_out", [4, D, D], F32, "ExternalInput")
    fg_d = dram("final_g", [D], F32, "ExternalInput")
    cos_d = dram("c_cos", [128, S], F32, "ExternalInput")
    t2_d = dram("c_t2", [128, S], F32, "ExternalInput")
    id_d = dram("c_ident", [128, 128], BF16, "ExternalInput")
    mk_d = dram("c_mask", [128, 1024], BF16, "ExternalInput")
    ka_d = dram("c_kaug", [16, S], BF16, "ExternalInput")
    pm_d = dram("c_pm", [128, 256], F32, "ExternalInput")
    ow_d = dram("c_own", [128, 256], F32, "ExternalInput")
    y_d = dram("y", [S, D], F32, "ExternalOutput")
    hs_d = y_d

    with contextlib.ExitStack() as es:
        sb = lambda n, s, dt: es.enter_context(nc.sbuf_tensor(n, s, dt))
        pp = lambda n, s, dt: es.enter_context(nc.psum_tensor(n, s, dt))
        win = sb("win", [128, 8, IN_B], BF16)
        wout = sb("wout", [128, 8, D], BF16)
        KT = sb("KT", [128, 4, S], BF16)
        Vcf = sb("Vcf", [128, NT * 4 * 65 + 64], BF16)
        Vc = Vcf[:, 0:NT * 4 * 65].rearrange("p (c k d) -> p c k d", c=NT, k=4)
        QT = sb("QT", [128, NH, 512], BF16)
        szT = sb("szT", [128, 8, 512], BF16)
        hnT = sb("hnT", [128, 8, 512], BF16)
        hl = [sb("hl%d" % i, [128, D], F32) for i in range(2)]
        hr = [sb("hr%d" % i, [128, D], F32) for i in range(2)]
        hn = [sb("hn%d" % i, [128, D], BF16) for i in range(2)]
        cosG = sb("cosG", [128, 512], F32)
        t2G = sb("t2G", [128, 512], F32)
        tAs = [sb("tA%d" % i, [128, 512], F32) for i in range(2)]
        tBs = [sb("tB%d" % i, [128, 512], F32) for i in range(2)]
        NPT = 4
        pT = [sb("pT%d" % i, [128, 512], BF16) for i in range(NPT)]
        maskT = sb("maskT", [128, 2, 512], BF16)
        ident = sb("ident", [128, 128], BF16)
        ones_bf = sb("ones_bf", [128, 128], BF16)
        gbc = sb("gbc", [128, D], F32)
        fgbc = sb("fgbc", [128, D], F32)
        szr = [sb("szr%d" % i, [128, 2, 128], BF16) for i in range(2)]
        gatedT = [sb("gatedT%d" % i, [128, 8, 128], BF16) for i in range(2)]
        gm = sb("gm", [128, NH, 16], F32)
        sel = sb("sel", [128, NH, 16], F32)
        m8 = sb("m8", [128, NH, 8], F32)
        biasb = sb("biasb", [128, NH, 16], BF16)
        pmT = sb("pmT", [128, 16, 16], F32)
        ownT = sb("ownT", [128, 16, 16], F32)
        kms = sb("kms", [64, 4, 2], F32)
        kmT = sb("kmT", [64, 4, 16], BF16)
        es_raw = sb("es_raw", [128, NH], F32)
        es_t = sb("es_t", [128, NH], F32)
        es_bf = sb("es_bf", [128, NH], BF16)
        e64 = sb("e64", [128, 128], BF16)
        ss = sb("ss", [128, 8], F32)
        nrm = sb("nrm", [128, 512], F32)
        rr = [sb("rr%d" % i, [128, 512], BF16) for i in range(4)]
        orow = sb("orow", [128, 128], BF16)
        NOSB = 4
        osb = [sb("osb%d" % i, [128, 512], F32) for i in range(NOSB)]

        psT = pp("psT", [128, 1024], BF16)
        psIP = [pp("psIP%d" % i, [128, 512], F32) for i in range(2)]
        psS = [pp("psS%d" % i, [128, 512], F32) for i in range(2)]
        psO = [pp("psO%d" % i, [128, 512], F32) for i in range(2)]
        psG = pp("psG", [128, 512], F32)

        P = Prog()
        op = P.op

        WIN_KEYS = ["win%d" % k for k in range(8)]
        WOUT_KEYS = ["wout%d" % k for k in range(8)]
        KTAUG = ["KTaug%d" % g for g in range(4)]
        op("sp", lambda e: e.dma_start(out=ident[:], in_=id_d[:, :]), writes=["ident"], dma="c0")
        op("sp", lambda e: e.dma_start(out=maskT[:].rearrange("p a b -> p (a b)"), in_=mk_d[:, :]), writes=["maskT"], dma="c1")
        op("sp", lambda e: e.dma_start(out=pmT[:].rearrange("p a b -> p (a b)"), in_=pm_d[:, :]), writes=["pmT"], dma="c2")
        op("sp", lambda e: e.dma_start(out=ownT[:].rearrange("p a b -> p (a b)"), in_=ow_d[:, :]), writes=["ownT"], dma="c3")
        op("sp", lambda e: e.dma_start(out=fgbc[:], in_=fg_d.partition_broadcast(128)), writes=["fgbc"], dma="c5")
        op("dve", lambda e: e.memset(ones_bf[:], 1.0), writes=["ones_bf"])
        op("dve", lambda e: e.memset(Vcf[:], 0.0), writes=["Vones"])
        op("dve", lambda e: e.memset(Vcf[:, 0:NT * 4 * 65].rearrange("p (c d) -> p c d", d=65)[:, :, 64:65], 1.0), reads=["Vones"], writes=["Vones"])
        op("dve", lambda e: e.memset(KT[64:128, :, :], 0.0), writes=KTAUG + ["KTpad"])
        op("dve", lambda e: e.memset(QT[64:128, :, :], 0.0), writes=["QTa%d" % i for i in range(4)])
        for g in range(4):
            op("sp", lambda e, g=g: e.dma_start(out=KT[64:80, g, :], in_=ka_d[:, :]), writes=["KTaug%d" % g], dma="c4_%d" % g)
        op("dve", lambda e: e.memset(kmT[:], 0.0), writes=["kmT"])
        op("dve", lambda e: e.memset(e64[:], 0.0), writes=["e64"])
        op("dve", lambda e: e.memset(es_bf[:], 0.0), writes=["es_bf"])
        op("dve", lambda e: e.memset(orow[:], 0.0), writes=["orow"])
        op("dve", lambda e: e.memset(orow[0:1, :], 1.0), reads=["orow"], writes=["orow"])
        for i_ in range(4):
            op("dve", lambda e, i_=i_: e.memset(rr[i_][:], 0.0), writes=["rr%d" % i_])
        op("dve", lambda e: e.memset(e64[0:1, 64:65], 1.0), reads=["e64"], writes=["e64"])

        out_ops = []

        win_loaded = set()
        stepN_prefetched = set()

        def load_win(li2):
            L2 = layers[li2]
            b2 = (L2 % 2 == 1)
            j2 = L2 // 2
            IN2 = IN_B if b2 else IN_A
            wd2 = wb_d if b2 else wa_d
            for kc in range(8):
                op("pool", lambda e, kc=kc: e.dma_start(out=win[:, kc, 0:IN2], in_=wd2[j2, kc * 128:(kc + 1) * 128, :]),
                   writes=[WIN_KEYS[kc]], dma="wi%d" % kc)
            win_loaded.add(li2)

        def emit_layer(li, L):
            is_b = (L % 2 == 1)
            j = L // 2
            nkv = 4 if is_b else 2
            IN = IN_B if is_b else IN_A
            kvw = nkv * HD
            koff, voff, zoff = 1024, 1024 + kvw, 1024 + 2 * kvw
            w_d = wb_d if is_b else wa_d
            KK = 128
            src_d = x_d if li == 0 else hs_d
            dst_d = y_d if li == len(layers) - 1 else hs_d
            last = (li == len(layers) - 1)
            do_fn = last and final_norm

            if li not in win_loaded:
                load_win(li)
            def load_wout():
                for kc in range(8):
                    op("pool", lambda e, kc=kc: e.dma_start(out=wout[:, kc, :], in_=wo_d[L, kc * 128:(kc + 1) * 128, :]),
                       writes=[WOUT_KEYS[kc]], dma="wo%d" % kc)
            if li != 0:
                load_wout()
            if li not in stepN_prefetched:
                op("sp", lambda e: e.dma_start(out=gbc[:], in_=ng_d[L].partition_broadcast(128)), writes=["gbc"], dma="g")
            if not is_b:
                op("dve", lambda e: e.memset(QT[64:128, :, :], 0.0), writes=["QTa%d" % i for i in range(4)])
                op("sp", lambda e: e.dma_start(out=es_raw[:], in_=sk_d[j].partition_broadcast(128)), writes=["es_raw"], dma="sk")
                op("act", lambda e: e.activation(out=es_t[:], in_=es_raw[:], func=AF.Exp), reads=["es_raw"], writes=["es_t"])
                op("dve", lambda e: e.tensor_copy(out=es_bf[0:1, :], in_=es_t[0:1, :]), reads=["es_t"], writes=["es_bf"])

            def rms_stats(src, src_key, col):
                c = ss[:, col:col + 1]
                if is_b:
                    op("dve", lambda e: e.scalar_tensor_tensor(out=tAs[0][:].bitcast(BF16), in0=src, scalar=1.0, in1=src, op0=ALU.mult, op1=ALU.mult, accum_out=c),
                       reads=[src_key], writes=["tA0", "ss%d" % col])
                else:
                    op("act", lambda e: e.activation(out=tAs[0][:].bitcast(BF16), in_=src, func=AF.Square, accum_out=c),
                       reads=[src_key], writes=["tA0", "ss%d" % col])
                op("dve", lambda e: e.tensor_scalar(out=c, in0=c, scalar1=1.0 / D, scalar2=EPS, op0=ALU.mult, op1=ALU.add),
                   reads=["ss%d" % col], writes=["ss%d" % col])
                op("act", lambda e: e.activation(out=c, in_=c, func=AF.Ln), reads=["ss%d" % col], writes=["ss%d" % col])
                op("act", lambda e: e.activation(out=c, in_=c, func=AF.Exp, scale=-0.5), reads=["ss%d" % col], writes=["ss%d" % col])
                return c

            ipb = [0]

            def next_ip():
                ipb[0] ^= 1
                return ipb[0]

            rope_cnt = [0]

            def rope_tile(pk, ps, dst0, dst1, dk0, dk1):
                ri = rope_cnt[0] % 2
                rope_cnt[0] += 1
                tA, tB = tAs[ri], tBs[ri]
                ka = "tA%d" % ri
                kb_ = "tB%d_" % ri
                op("dve", lambda e: e.tensor_tensor(out=tA[:], in0=ps[:], in1=cosG[:], op=ALU.mult), reads=[pk, "cosG"], writes=[ka])
                for (o0, i0) in ((0, 32), (32, 0), (64, 96), (96, 64)):
                    op("dve", lambda e, o0=o0, i0=i0: e.tensor_tensor(out=tB[o0:o0 + 32, :], in0=ps[i0:i0 + 32, :], in1=t2G[o0:o0 + 32, :], op=ALU.mult),
                       reads=[pk, "t2G"], writes=[kb_ + str(o0)])
                op(ROPE_ADD_ENG, lambda e: e.tensor_tensor(out=dst0, in0=tA[0:64, :], in1=tB[0:64, :], op=ALU.add),
                   reads=[ka, kb_ + "0", kb_ + "32"], writes=[dk0])
                op(ROPE_ADD_ENG, lambda e: e.tensor_tensor(out=dst1, in0=tA[64:128, :], in1=tB[64:128, :], op=ALU.add),
                   reads=[ka, kb_ + "64", kb_ + "96"], writes=[dk1])

            def emit_stepN_dma(G, tt, src=None):
                t = 4 * G + tt
                sl = t % 2
                src = src_d if src is None else src
                op("sp", lambda e: e.dma_start(out=hl[sl][:], in_=src[t * 128:(t + 1) * 128, :]),
                   reads=["hd%d" % t], writes=["hl%d" % sl], dma="hl%d" % sl)

            def emit_stepN_a1(G, tt):
                t = 4 * G + tt
                sl = t % 2
                c = ss[:, 4 + tt:5 + tt]
                sk = "ssN%d" % tt
                src = hl[sl][:]
                if is_b:
                    op("dve", lambda e: e.scalar_tensor_tensor(out=tAs[0][:].bitcast(BF16), in0=src, scalar=1.0, in1=src, op0=ALU.mult, op1=ALU.mult, accum_out=c),
                       reads=["hl%d" % sl], writes=["tA0", sk])
                else:
                    op("act", lambda e: e.activation(out=tAs[0][:].bitcast(BF16), in_=src, func=AF.Square, accum_out=c),
                       reads=["hl%d" % sl], writes=["tA0", sk])
                op("dve", lambda e: e.tensor_scalar(out=c, in0=c, scalar1=1.0 / D, scalar2=EPS, op0=ALU.mult, op1=ALU.add),
                   reads=[sk], writes=[sk])

            def emit_stepN_a2(G, tt):
                c = ss[:, 4 + tt:5 + tt]
                sk = "ssN%d" % tt
                op("act", lambda e: e.activation(out=c, in_=c, func=AF.Ln), reads=[sk], writes=[sk])
                op("act", lambda e: e.activation(out=c, in_=c, func=AF.Exp, scale=-0.5), reads=[sk], writes=[sk])

            def emit_stepN_a3(G, tt):
                t = 4 * G + tt
                sl = t % 2
                c = ss[:, 4 + tt:5 + tt]
                op("dve", lambda e: e.scalar_tensor_tensor(out=hn[sl][:], in0=hl[sl][:], scalar=c, in1=gbc[:], op0=ALU.mult, op1=ALU.mult),
                   reads=["hl%d" % sl, "ssN%d" % tt, "gbc"], writes=["hn%d" % sl])

            def emit_stepN_a(G, tt):
                emit_stepN_dma(G, tt)
                emit_stepN_a1(G, tt)
                emit_stepN_a2(G, tt)
                emit_stepN_a3(G, tt)

            def emit_stepN_b(G, tt):
                t = 4 * G + tt
                sl = t % 2
                for kc in range(8):
                    op("pe", lambda e, kc=kc: e.transpose(out=psT[:, kc * 128:(kc + 1) * 128], in_=hn[sl][:, kc * 128:(kc + 1) * 128], identity=ident[:]),
                       reads=["hn%d" % sl, "ident"], writes=["psT"])
                if is_b:
                    op("dve", lambda e: e.tensor_copy(out=hnT[:, :, tt * 128:(tt + 1) * 128], in_=psT[:].rearrange("p (k t) -> p k t", k=8)),
                       reads=["psT"], writes=["hnT%d" % tt])
                else:
                    op("act", lambda e: e.activation(out=hnT[:, :, tt * 128:(tt + 1) * 128], in_=psT[:].rearrange("p (k t) -> p k t", k=8), func=AF.Copy),
                       reads=["psT"], writes=["hnT%d" % tt])

            def emit_stepN_tile(G, tt):
                emit_stepN_a(G, tt)
                emit_stepN_b(G, tt)

            def emit_group(G):
                tok0 = 512 * G
                HNT = ["hnT%d" % tt for tt in range(4)]
                op("sp", lambda e, tok0=tok0: e.dma_start(out=cosG[:], in_=cos_d[:, tok0:tok0 + 512]), writes=["cosG"], dma="cos")
                op("sp", lambda e, tok0=tok0: e.dma_start(out=t2G[:], in_=t2_d[:, tok0:tok0 + 512]), writes=["t2G"], dma="t2")

                IPB = {"psIP0": psIP[0], "psIP1": psIP[1], "psS0": psS[0], "psS1": psS[1]}
                rb = [0]
                zb = [0]

                def proj_fm(col0, bk):
                    for kc in range(8):
                        op("pe", lambda e, kc=kc: e.matmul(IPB[bk][:], lhsT=win[:, kc, col0:col0 + 128], rhs=hnT[:, kc, :], start=(kc == 0), stop=(kc == 7)),
                           reads=WIN_KEYS + HNT, writes=[bk])

                def rope_bank():
                    rb[0] ^= 1
                    return "psIP%d" % rb[0]

                def z_bank():
                    zb[0] ^= 1
                    return "psS%d" % zb[0]

                for f in range(8):
                    bk = rope_bank()
                    proj_fm(f * 128, bk)
                    rope_tile(bk, IPB[bk], QT[0:64, 2 * f, :], QT[0:64, 2 * f + 1, :], "QT%d" % (2 * f), "QT%d" % (2 * f + 1))
                    if f == 0 and tail_box[0] is not None:
                        tail_box[0]()
                        tail_box[0] = None
                    bz = z_bank()
                    proj_fm(zoff + f * 128, bz)
                    op("act", lambda e, bz=bz, f=f: e.activation(out=szT[:, f, :], in_=IPB[bz][:], func=AF.Silu),
                       reads=[bz], writes=["szT%d" % f])
                    if f < nkv // 2:
                        kt = f
                        bk = rope_bank()
                        proj_fm(koff + kt * 128, bk)
                        rope_tile(bk, IPB[bk], KT[0:64, 2 * kt, tok0:tok0 + 512], KT[0:64, 2 * kt + 1, tok0:tok0 + 512],
                                  "KT%d_%d" % (2 * kt, G), "KT%d_%d" % (2 * kt + 1, G))
                    if f < 4:
                        tt = f
                        t = 4 * G + tt
                        bz = z_bank()
                        for kc in range(8):
                            op("pe", lambda e, bz=bz, kc=kc, tt=tt: e.matmul(IPB[bz][:, 0:kvw], lhsT=hnT[:, kc, tt * 128:(tt + 1) * 128], rhs=win[:, kc, voff:voff + kvw], start=(kc == 0), stop=(kc == 7)),
                               reads=WIN_KEYS + HNT, writes=[bz])
                        op("dve", lambda e, bz=bz, t=t: e.tensor_copy(out=Vc[:, t, 0:nkv, 0:64], in_=IPB[bz][:, 0:kvw].rearrange("p (k d) -> p k d", k=nkv)),
                           reads=[bz], writes=["V%d" % t])
                def stepG_a0(tt):
                    t = 4 * G + tt
                    qb = t // 2
                    qs = slice(tt * 128, (tt + 1) * 128)
                    QTK = ["QT%d" % h for h in range(NH)]
                    for h in range(NH):
                        op("pe", lambda e, h=h: e.matmul(psG[:, h * 16:(h + 1) * 16], lhsT=QT[0:64, h, qs], rhs=kmT[:, h // 4, :], start=True, stop=True),
                           reads=QTK + ["kmT"], writes=["psG"])
                    op("dve", lambda e: e.tensor_tensor(out=gm[:], in0=psG[:, 0:256].rearrange("p (h j) -> p h j", h=NH),
                                                        in1=pmT[:, qb:qb + 1, :].to_broadcast([128, NH, 16]), op=ALU.add),
                       reads=["psG", "pmT"], writes=["gm"])

                def stepG_max(tt, h0):
                    for h in range(h0, h0 + 4):
                        op("dve", lambda e, h=h: e.max(out=m8[:, h, :], in_=gm[:, h, :]), reads=["gm"], writes=["m8_%d" % h])

                def stepG_sel(tt):
                    op("dve", lambda e: e.tensor_tensor(out=sel[:], in0=gm[:], in1=m8[:, :, 2:3].to_broadcast([128, NH, 16]), op=ALU.is_ge),
                       reads=["gm"] + ["m8_%d" % h for h in range(NH)], writes=["sel"])

                def stepG_bias(tt):
                    qb = (4 * G + tt) // 2
                    op("dve", lambda e: e.scalar_tensor_tensor(out=biasb[:], in0=sel[:], scalar=-1.0, in1=ownT[:, qb:qb + 1, :].to_broadcast([128, NH, 16]),
                                                               op0=ALU.add, op1=ALU.add),
                       reads=["sel", "ownT"], writes=["biasb"])

                def stepG_b(tt, half):
                    qs = slice(tt * 128, (tt + 1) * 128)
                    for hh in range(8):
                        op("pe", lambda e, hh=hh: e.transpose(out=psT[0:16, hh * 128:(hh + 1) * 128], in_=biasb[:, half * 8 + hh, :], identity=ident[:]),
                           reads=["biasb", "ident"], writes=["psT"])
                    op("dve", lambda e: e.tensor_copy(out=QT[64:80, half * 8:(half + 1) * 8, qs], in_=psT[0:16, :].rearrange("p (h q) -> p h q", h=8)),
                       reads=["psT"], writes=["QTa%d" % tt])

                def emit_stepG_a(tt):
                    qb = (4 * G + tt) // 2
                    qs = slice(tt * 128, (tt + 1) * 128)
                    if qb < 4:
                        op("dve", lambda e: e.memset(QT[64:80, :, qs], 0.0), writes=["QTa%d" % tt])
                        return False
                    stepG_a0(tt)
                    for h0 in range(0, NH, 4):
                        stepG_max(tt, h0)
                    stepG_sel(tt)
                    stepG_bias(tt)
                    return True

                def emit_stepG_b(tt):
                    stepG_b(tt, 0)
                    stepG_b(tt, 1)

                def emit_stepG_tile(tt):
                    if emit_stepG_a(tt):
                        emit_stepG_b(tt)

                if li == 0 and G == 0:
                    load_wout()
                if G == DEBUG_NG - 1 and li + 1 < len(layers):
                    load_win(li + 1)
                if is_b:
                    KTG = ["KT%d_%d" % (g, G) for g in range(4)]
                    op("dve", lambda e, tok0=tok0: e.tensor_reduce(out=kms[:], in_=KT[0:64, :, tok0:tok0 + 512].rearrange("p k (b s) -> p k b s", b=2), axis=AX.X, op=ALU.add),
                       reads=KTG, writes=["kms"])
                    op("dve", lambda e, G=G: e.tensor_scalar(out=kmT[:, :, 2 * G:2 * G + 2], in0=kms[:], scalar1=1.0 / 256, scalar2=None, op0=ALU.mult),
                       reads=["kms"], writes=["kmT"])
                    emit_stepG_tile(0)
                    if 4 * G // 2 < 4:
                        for tt_ in range(1, 4):
                            emit_stepG_tile(tt_)
                units = []
                for tt in range(4):
                    t = 4 * G + tt
                    if is_b:
                        chunks = [(c, None) for c in range(t)] + [(t, 0)]
                    else:
                        chunks = ([(t - 1, 1)] if t >= 1 else []) + [(t, 0)]
                    for g4 in range(4):
                        for i, (c, mk) in enumerate(chunks):
                            units.append((tt, g4, i, c, mk, i == len(chunks) - 1))
                nu = len(units)
                SB = [psS[0], psS[1], psIP[1]]
                SBK = ["psS0", "psS1", "psIP1"]
                LA = 2
                pending = []

                def defer(due, chain, stage, fn):
                    pending.append((due, chain, stage, fn))

                def run_pending(u, max_stage=10, min_stage=0, max_chain=1 << 30):
                    while True:
                        cand = [p for p in pending if p[0] <= u and min_stage <= p[2] <= max_stage and p[1] <= max_chain]
                        if not cand:
                            return
                        cand.sort(key=lambda p: (p[1], p[2]))
                        pending.remove(cand[0])
                        cand[0][3]()

                DL1, DL2, DL3 = 3, 5, 8
                first_unit = {}
                for ui, un in enumerate(units):
                    first_unit.setdefault(un[0], ui)
                nxt_layer = (G + 1 == DEBUG_NG) and (li + 1 < len(layers))
                if nxt_layer:
                    op("sp", lambda e: e.dma_start(out=gbc[:], in_=ng_d[layers[li + 1]].partition_broadcast(128)), writes=["gbc"], dma="g")
                    stepN_prefetched.add(li + 1)
                if G + 1 < DEBUG_NG or nxt_layer:
                    Gn = (G + 1) if not nxt_layer else 0
                    srcn = None if not nxt_layer else hs_d
                    for tt_ in range(4):
                        if tt_ == 0:
                            ddue = 0
                        elif tt_ == 1:
                            ddue = 1
                        else:
                            ddue = max(first_unit[tt_ - 1] + 1, first_unit[tt_ - 2] + 7)
                        defer(ddue, tt_ * 4 - 3.9, 0, lambda tt_=tt_: emit_stepN_dma(Gn, tt_, srcn))
                        n_tile = sum(1 for un in units if un[0] == tt_)
                        so = 11 if n_tile >= 28 else 0
                        defer(first_unit[tt_] + 1 + so, tt_ * 4 + 0.25, 0, lambda tt_=tt_: emit_stepN_a1(Gn, tt_))
                        defer(first_unit[tt_] + 4 + so, tt_ * 4 + 0.26, 0, lambda tt_=tt_: emit_stepN_a2(Gn, tt_))
                        defer(first_unit[tt_] + 6 + so, tt_ * 4 + 0.27, 0, lambda tt_=tt_: emit_stepN_a3(Gn, tt_))
                        defer(first_unit[tt_] + 10 + so, tt_ * 4 + 0.75, 0, lambda tt_=tt_: emit_stepN_b(Gn, tt_))
                if is_b and (4 * G) // 2 >= 4:
                    for tt_ in range(3):
                        fu = first_unit[tt_]
                        defer(fu + 3, tt_ * 4 + 0.50, 0, lambda tt_=tt_: stepG_a0(tt_ + 1))
                        for i_ in range(4):
                            defer(fu + 5 + i_, tt_ * 4 + 0.51 + 0.01 * i_, 0, lambda tt_=tt_, i_=i_: stepG_max(tt_ + 1, 4 * i_))
                        defer(fu + 9, tt_ * 4 + 0.56, 0, lambda tt_=tt_: stepG_sel(tt_ + 1))
                        defer(fu + 10, tt_ * 4 + 0.57, 0, lambda tt_=tt_: stepG_bias(tt_ + 1))
                        defer(fu + 14, tt_ * 4 + 0.60, 0, lambda tt_=tt_: stepG_b(tt_ + 1, 0))
                        defer(fu + 17, tt_ * 4 + 0.61, 0, lambda tt_=tt_: stepG_b(tt_ + 1, 1))

                def emit_qk(u):
                    tt, g4, i, c, mk, lastc = units[u]
                    kvh = g4 if is_b else g4 // 2
                    sbk = u % 3
                    qs = slice(tt * 128, (tt + 1) * 128)
                    rd = ["KT%d_%d" % (kvh, c // 4)] + ["QT%d" % h for h in range(4 * g4, 4 * g4 + 4)]
                    rd += ["QTa%d" % tt] + KTAUG
                    op("pe", lambda e: e.matmul(SB[sbk][:], lhsT=KT[0:KK, kvh, c * 128:(c + 1) * 128], rhs=QT[0:KK, 4 * g4:4 * g4 + 4, qs], start=True, stop=(mk is None)),
                       reads=rd, writes=[SBK[sbk]])
                    if mk is not None:
                        op("pe", lambda e: e.matmul(SB[sbk][:], lhsT=ident[:], rhs=maskT[:, mk, :], start=False, stop=True),
                           reads=["ident", "maskT"], writes=[SBK[sbk]])

                def emit_exp_pv(u):
                    tt, g4, i, c, mk, lastc = units[u]
                    kvh = g4 if is_b else g4 // 2
                    sbk = u % 3
                    ps_ = u % NPT
                    ob = (tt * 4 + g4) % 2
                    op("act", lambda e: e.activation(out=pT[ps_][:], in_=SB[sbk][:], func=AF.Exp, scale=0.125),
                       reads=[SBK[sbk]], writes=["pT%d" % ps_])
                    op("pe", lambda e: e.matmul(psO[ob][:, :], lhsT=Vcf[:, (c * 4 + kvh) * 65:(c * 4 + kvh) * 65 + 128], rhs=pT[ps_][:], start=(i == 0), stop=(lastc and is_b)),
                       reads=["V%d" % c, "Vones", "pT%d" % ps_], writes=["psO%d" % ob])
                    if lastc and not is_b:
                        op("pe", lambda e: e.matmul(psO[ob][:, :], lhsT=e64[:, :], rhs=es_bf[:, 4 * g4:4 * g4 + 4].unsqueeze(2).to_broadcast([128, 4, 128]), start=False, stop=True),
                           reads=["e64", "es_bf"], writes=["psO%d" % ob])
                    if lastc:
                        ok = "psO%d" % ob
                        chain = tt * 4 + g4
                        kb = chain % NOSB
                        run_pending(1 << 30, max_stage=2, min_stage=1, max_chain=chain - NOSB)
                        op("dve", lambda e: e.tensor_copy(out=osb[kb][0:65, :], in_=psO[ob][0:65, :]), reads=[ok], writes=["osb%d" % kb])
                        defer(u + DL1, chain, 1, lambda: norm_stage1(u, tt, g4))

                def norm_stage1(u0, tt, g4):
                    kb = (tt * 4 + g4) % NOSB
                    op("act", lambda e: e.activation(out=nrm[32:33, :], in_=osb[kb][64:65, :], func=AF.Ln), reads=["osb%d" % kb], writes=["nrm_l"])
                    op("act", lambda e: e.activation(out=rr[kb][0:1, :], in_=nrm[32:33, :], func=AF.Exp, scale=-1.0), reads=["nrm_l"], writes=["rr%d" % kb])
                    defer(u0 + DL2, tt * 4 + g4, 2, lambda: norm_stage2(u0, tt, g4))

                def norm_stage2(u0, tt, g4):
                    qs = slice(tt * 128, (tt + 1) * 128)
                    gi = (4 * G + tt) % 2
                    kb = (tt * 4 + g4) % NOSB
                    op("pe", lambda e: e.matmul(psG[:], lhsT=orow[:, :], rhs=rr[kb][:, :], start=True, stop=True),
                       reads=["orow", "rr%d" % kb], writes=["psG"])
                    for par in range(2):
                        p0 = 64 * par
                        sz_in = szT[p0:p0 + 64, 2 * g4:2 * g4 + 2, qs]
                        rb_in = psG[p0:p0 + 64, :].rearrange("p (h q) -> p h q", h=4)[:, par::2, :]
                        o_in = osb[kb][0:64, :].rearrange("p (h q) -> p h q", h=4)[:, par::2, :]
                        op("dve", lambda e, sz_in=sz_in, rb_in=rb_in, par=par: e.tensor_tensor(out=szr[par][0:64, :, :], in0=sz_in, in1=rb_in, op=ALU.mult),
                           reads=["szT%d" % (2 * g4), "szT%d" % (2 * g4 + 1), "psG"], writes=["szr%d" % par])
                        op("dve", lambda e, o_in=o_in, par=par, p0=p0: e.tensor_tensor(out=gatedT[gi][p0:p0 + 64, 2 * g4:2 * g4 + 2, :], in0=o_in, in1=szr[par][0:64, :, :], op=ALU.mult),
                           reads=["osb%d" % kb, "szr%d" % par], writes=["gT%d_%d_%d" % (gi, g4, par)])
                    if g4 == 3:
                        defer(u0 + DL3, tt * 4 + g4, 3, lambda: emit_outproj(tt, u0 + DL3))

                def emit_outproj(tt, ub):
                    t = 4 * G + tt
                    gi = t % 2
                    sl = t % 2
                    chain = tt * 4 + 3
                    GT = ["gT%d_%d_%d" % (gi, g4, par) for g4 in range(4) for par in range(2)]
                    op("sp", lambda e: e.dma_start(out=hr[sl][:], in_=src_d[t * 128:(t + 1) * 128, :]),
                       reads=["hd%d" % t], writes=["hr%d" % sl], dma="hr%d" % sl)

                    def piece(n, f0):
                        for f in (f0, f0 + 1):
                            op("pe", lambda e, f=f: e.matmul(psIP[0][:], lhsT=gatedT[gi][:, f, :], rhs=wout[:, f, n * 512:(n + 1) * 512], start=(f == 0), stop=(f == 7)),
                               reads=GT + WOUT_KEYS, writes=["psIP0"])

                    def add_half(n):
                        op("dve", lambda e: e.tensor_tensor(out=hr[sl][:, n * 512:(n + 1) * 512], in0=psIP[0][:], in1=hr[sl][:, n * 512:(n + 1) * 512], op=ALU.add),
                           reads=["psIP0", "hr%d" % sl], writes=["hr%d" % sl])
                        if n == 1:
                            finish()

                    def finish():
                        if do_fn:
                            c = rms_stats(hr[sl][:], "hr%d" % sl, 1)
                            op("dve", lambda e, c=c: e.scalar_tensor_tensor(out=hr[sl][:], in0=hr[sl][:], scalar=c, in1=fgbc[:], op0=ALU.mult, op1=ALU.mult),
                               reads=["hr%d" % sl, "ss1", "fgbc"], writes=["hr%d" % sl])
                        o = op("pool", lambda e: e.dma_start(out=dst_d[t * 128:(t + 1) * 128, :], in_=hr[sl][:]),
                               reads=["hr%d" % sl], writes=["hd%d" % t], dma="st%d" % sl)
                        if last:
                            out_ops.append(o)

                    n_next = sum(1 for un in units if un[0] == tt + 1)
                    if n_next >= 14:
                        for n in range(2):
                            base = ub + n * 6
                            for k in range(4):
                                defer(base + k, chain, 3.0 + 0.01 * (n * 6 + k + 1), lambda n=n, k=k: piece(n, 2 * k))
                            defer(base + 5, chain, 3.0 + 0.01 * (n * 6 + 6), lambda n=n: add_half(n))
                    else:
                        OPB = [(psIP[0][:], "psIP0"), (psT[:].bitcast(F32), "psT")]
                        for n in range(2):
                            for f in range(8):
                                op("pe", lambda e, n=n, f=f: e.matmul(OPB[n][0], lhsT=gatedT[gi][:, f, :], rhs=wout[:, f, n * 512:(n + 1) * 512], start=(f == 0), stop=(f == 7)),
                                   reads=GT + WOUT_KEYS, writes=[OPB[n][1]])
                        for n in range(2):
                            op("dve", lambda e, n=n: e.tensor_tensor(out=hr[sl][:, n * 512:(n + 1) * 512], in0=OPB[n][0], in1=hr[sl][:, n * 512:(n + 1) * 512], op=ALU.add),
                               reads=[OPB[n][1], "hr%d" % sl], writes=["hr%d" % sl])
                        finish()

                for v in range(min(LA, nu)):
                    emit_qk(v)
                for u in range(nu):
                    if u + LA < nu:
                        emit_qk(u + LA)
                    emit_exp_pv(u)
                    run_pending(u)
                run_pending(1 << 30, max_stage=0.9)

                def flush_tail():
                    while pending:
                        run_pending(1 << 30)
                if G + 1 < DEBUG_NG:
                    tail_box[0] = flush_tail
                else:
                    flush_tail()

            tail_box = [None]
            if li not in stepN_prefetched:
                for tt in range(4):
                    emit_stepN_tile(0, tt)
            for G in range(DEBUG_NG):
                emit_group(G)

        for li, L in enumerate(layers):
            emit_layer(li, L)

        P.emit(nc, final_wait_ops=out_ops)
    return nc


_CONSTS = None
_PROGS = {}


def _get_prog(layers, final_norm):
    key = (tuple(layers), final_norm)
    if key not in _PROGS:
        _PROGS[key] = build_program(list(layers), final_norm)
    return _PROGS[key]


LAUNCH_PLAN = [([0, 1, 2, 3], True)]


def kernel(x, norm_g, w_in_a, sinks_a, w_in_b, w_out, final_g):
    global _CONSTS
    if _CONSTS is None:
        _CONSTS = _host_consts()
    n = 8
    x = np.ascontiguousarray(np.asarray(x, dtype=np.float32))
    shared = {"norm_g": np.ascontiguousarray(np.asarray(norm_g, np.float32)),
              "w_in_a": np.ascontiguousarray(np.asarray(w_in_a, np.float32)),
              "sinks_a": np.ascontiguousarray(np.asarray(sinks_a, np.float32)),
              "w_in_b": np.ascontiguousarray(np.asarray(w_in_b, np.float32)),
              "w_out": np.ascontiguousarray(np.asarray(w_out, np.float32)),
              "final_g": np.ascontiguousarray(np.asarray(final_g, np.float32))}
    shared.update(_CONSTS)
    cur = [x[b] for b in range(n)]
    for layers, fn in LAUNCH_PLAN:
        nc = _get_prog(layers, fn)
        in_maps = [dict(shared, x=cur[b]) for b in range(n)]
        res = run_bass_kernel_spmd(nc, in_maps, core_ids=list(range(n)))
        cur = [np.asarray(r["y"]) for r in res.results]
    return np.stack(cur, 0).astype(np.float32)
```

```python
import contextlib
import numpy as np
import ml_dtypes
import concourse.bass as bass
import concourse.mybir as mybir
from concourse.bass_utils import run_bass_kernel_spmd

F32 = mybir.dt.float32
BF16 = mybir.dt.bfloat16
ALU = mybir.AluOpType
AF = mybir.ActivationFunctionType
AX = mybir.AxisListType

S = 4096
D = 1024
NH = 16
HD = 64
NT = S // 128
NG = S // 512
IN_A = 2304
IN_B = 2560
EPS = 1e-5
BIG = 30000.0
DEBUG_NG = NG
ROPE_ADD_ENG = "pool"
STRICT_SAME_ENGINE = False

ENGS = ("pe", "act", "dve", "pool", "sp")


class Op:
    __slots__ = ("eng", "fn", "idx", "waits", "signal", "sigval", "dkey", "dval", "is_dma")

    def __init__(self, eng, fn):
        self.eng = eng
        self.fn = fn
        self.idx = -1
        self.waits = []
        self.signal = False
        self.sigval = 0
        self.dkey = None
        self.dval = 0
        self.is_dma = False


class Prog:
    def __init__(self, same_engine_raw=True):
        self.streams = {e: [] for e in ENGS}
        self.reg = {}
        self.waited = {e: {} for e in ENGS}
        self.dma_count = {}
        self.same_engine_raw = same_engine_raw

    def _add_dep(self, op, dep, kind):
        if dep is None or dep is op:
            return
        if dep.is_dma:
            w = self.waited[op.eng]
            if w.get(("d", dep.dkey), 0) >= dep.dval:
                return
            w[("d", dep.dkey)] = dep.dval
            op.waits = [x for x in op.waits if not (x.is_dma and x.dkey == dep.dkey)]
            op.waits.append(dep)
            return
        if dep.eng == op.eng and not op.is_dma:
            if op.eng == "pe":
                return
            if (kind != "raw" and not STRICT_SAME_ENGINE) or not self.same_engine_raw:
                return
        w = self.waited[op.eng]
        if w.get(dep.eng, -1) >= dep.idx:
            return
        w[dep.eng] = dep.idx
        dep.signal = True
        op.waits = [x for x in op.waits if x.is_dma or x.eng != dep.eng]
        op.waits.append(dep)

    def op(self, eng, fn, reads=(), writes=(), dma=None):
        writes = list(writes) + [k for k in reads if k.startswith("ps")]
        reads = [k for k in reads if not k.startswith("ps")]
        o = Op(eng, fn)
        o.idx = len(self.streams[eng])
        if dma is not None:
            o.is_dma = True
            o.dkey = dma
            self.dma_count[dma] = self.dma_count.get(dma, 0) + 1
            o.dval = 16 * self.dma_count[dma]
        for r in reads:
            st = self.reg.get(r)
            if st is not None:
                self._add_dep(o, st[0], "raw")
        for w in writes:
            st = self.reg.get(w)
            if st is not None:
                self._add_dep(o, st[0], "waw")
                for rd in st[1]:
                    self._add_dep(o, rd, "war")
        for w in writes:
            self.reg[w] = [o, []]
        for r in reads:
            st = self.reg.get(r)
            if st is None:
                self.reg[r] = [None, [o]]
            else:
                st[1].append(o)
        self.streams[eng].append(o)
        return o

    def emit(self, nc, final_wait_ops=()):
        for e in ENGS:
            c = 0
            for o in self.streams[e]:
                if o.is_dma:
                    continue
                if o.signal:
                    c += 1
                    o.sigval = c
        dkeys = sorted(self.dma_count.keys())
        with contextlib.ExitStack() as es:
            esem = {e: es.enter_context(nc.semaphore("sem_" + e)) for e in ENGS}
            dsem = {k: es.enter_context(nc.semaphore("dsem_" + k)) for k in dkeys}
            block = es.enter_context(nc.Block())
            streams = self.streams

            def run(e, eng):
                for o in streams[e]:
                    for d in o.waits:
                        if d.is_dma:
                            eng.wait_ge(dsem[d.dkey], d.dval)
                        else:
                            eng.wait_ge(esem[d.eng], d.sigval)
                    ins = o.fn(eng)
                    if o.is_dma:
                        ins.then_inc(dsem[o.dkey], 16)
                    elif o.signal:
                        ins.then_inc(esem[e], 1)
                if e == "sp":
                    done = {}
                    for d in final_wait_ops:
                        done[d.dkey] = max(done.get(d.dkey, 0), d.dval)
                    for k, v in done.items():
                        eng.wait_ge(dsem[k], v)

            @block.tensor
            def _(eng):
                run("pe", eng)

            @block.scalar
            def _(eng):
                run("act", eng)

            @block.vector
            def _(eng):
                run("dve", eng)

            @block.gpsimd
            def _(eng):
                run("pool", eng)

            @block.sync
            def _(eng):
                run("sp", eng)


def _host_consts():
    bf = ml_dtypes.bfloat16
    pos = np.arange(S, dtype=np.float32)
    inv = (10000.0 ** (-np.arange(0, HD, 2, dtype=np.float32) / HD)).astype(np.float32)
    ang = (pos[None, :] * inv[:, None]).astype(np.float32)
    c32, s32 = np.cos(ang).astype(np.float32), np.sin(ang).astype(np.float32)
    cosT = np.concatenate([c32, c32, c32, c32], 0)
    t2T = np.concatenate([-s32, s32, -s32, s32], 0)
    ident = np.eye(128, dtype=np.float32).astype(bf)
    kk = np.arange(128)[:, None]
    qq = np.arange(128)[None, :]
    cur = np.where(kk <= qq, 0.0, -BIG).astype(np.float32)
    prev = np.where(kk > qq, 0.0, -BIG).astype(np.float32)
    maskT = np.stack([np.tile(cur, (1, 4)), np.tile(prev, (1, 4))], 1).astype(bf)
    kaug = np.where((np.arange(S)[None, :] // 256) == np.arange(16)[:, None], BIG, 0.0).astype(np.float32).astype(bf)
    qb = np.arange(16)[:, None]
    jj = np.arange(16)[None, :]
    pm = np.where(jj < qb, 0.0, -1e30).astype(np.float32)
    own = (jj == qb).astype(np.float32)
    pmT = np.broadcast_to(pm[None], (128, 16, 16)).copy()
    ownT = np.broadcast_to(own[None], (128, 16, 16)).copy()
    return {"c_cos": cosT, "c_t2": t2T, "c_ident": ident, "c_mask": maskT.reshape(128, 1024),
            "c_kaug": kaug, "c_pm": pmT.reshape(128, 256), "c_own": ownT.reshape(128, 256)}


def build_program(layers, final_norm):
    nc = bass.Bass("TRN2", target_bir_lowering=False)
    dram = lambda n, s, dt, k: nc.dram_tensor(n, s, dt, kind=k).ap()
    x_d = dram("x", [S, D], F32, "ExternalInput")
    ng_d = dram("norm_g", [4, D], F32, "ExternalInput")
    wa_d = dram("w_in_a", [2, D, IN_A], F32, "ExternalInput")
    sk_d = dram("sinks_a", [2, NH], F32, "ExternalInput")
    wb_d = dram("w_in_b", [2, D, IN_B], F32, "ExternalInput")
    wo_d = dram("w_out", [4, D, D], F32, "ExternalInput")
    fg_d = dram("final_g", [D], F32, "ExternalInput")
    cos_d = dram("c_cos", [128, S], F32, "ExternalInput")
    t2_d = dram("c_t2", [128, S], F32, "ExternalInput")
    id_d = dram("c_ident", [128, 128], BF16, "ExternalInput")
    mk_d = dram("c_mask", [128, 1024], BF16, "ExternalInput")
    ka_d = dram("c_kaug", [16, S], BF16, "ExternalInput")
    pm_d = dram("c_pm", [128, 256], F32, "ExternalInput")
    ow_d = dram("c_own", [128, 256], F32, "ExternalInput")
    y_d = dram("y", [S, D], F32, "ExternalOutput")
    hs_d = y_d

    with contextlib.ExitStack() as es:
        sb = lambda n, s, dt: es.enter_context(nc.sbuf_tensor(n, s, dt))
        pp = lambda n, s, dt: es.enter_context(nc.psum_tensor(n, s, dt))
        win = sb("win", [128, 8, IN_B], BF16)
        wout = sb("wout", [128, 8, D], BF16)
        KT = sb("KT", [128, 4, S], BF16)
        Vcf = sb("Vcf", [128, NT * 4 * 65 + 64], BF16)
        Vc = Vcf[:, 0:NT * 4 * 65].rearrange("p (c k d) -> p c k d", c=NT, k=4)
        QT = sb("QT", [128, NH, 512], BF16)
        szT = sb("szT", [128, 8, 512], BF16)
        hnT = sb("hnT", [128, 8, 512], BF16)
        hl = [sb("hl%d" % i, [128, D], F32) for i in range(2)]
        hr = [sb("hr%d" % i, [128, D], F32) for i in range(2)]
        hn = [sb("hn%d" % i, [128, D], BF16) for i in range(2)]
        cosG = sb("cosG", [128, 512], F32)
        t2G = sb("t2G", [128, 512], F32)
        tAs = [sb("tA%d" % i, [128, 512], F32) for i in range(2)]
        tBs = [sb("tB%d" % i, [128, 512], F32) for i in range(2)]
        NPT = 4
        pT = [sb("pT%d" % i, [128, 512], BF16) for i in range(NPT)]
        maskT = sb("maskT", [128, 2, 512], BF16)
        ident = sb("ident", [128, 128], BF16)
        ones_bf = sb("ones_bf", [128, 128], BF16)
        gbc = sb("gbc", [128, D], F32)
        fgbc = sb("fgbc", [128, D], F32)
        szr = [sb("szr%d" % i, [128, 2, 128], BF16) for i in range(2)]
        gatedT = [sb("gatedT%d" % i, [128, 8, 128], BF16) for i in range(2)]
        gm = sb("gm", [128, NH, 16], F32)
        sel = sb("sel", [128, NH, 16], F32)
        m8 = sb("m8", [128, NH, 8], F32)
        biasb = sb("biasb", [128, NH, 16], BF16)
        pmT = sb("pmT", [128, 16, 16], F32)
        ownT = sb("ownT", [128, 16, 16], F32)
        kms = sb("kms", [64, 4, 2], F32)
        kmT = sb("kmT", [64, 4, 16], BF16)
        es_raw = sb("es_raw", [128, NH], F32)
        es_t = sb("es_t", [128, NH], F32)
        es_bf = sb("es_bf", [128, NH], BF16)
        e64 = sb("e64", [128, 128], BF16)
        ss = sb("ss", [128, 8], F32)
        nrm = sb("nrm", [128, 512], F32)
        rr = [sb("rr%d" % i, [128, 512], BF16) for i in range(4)]
        orow = sb("orow", [128, 128], BF16)
        NOSB = 4
        osb = [sb("osb%d" % i, [128, 512], F32) for i in range(NOSB)]

        psT = pp("psT", [128, 1024], BF16)
        psIP = [pp("psIP%d" % i, [128, 512], F32) for i in range(2)]
        psS = [pp("psS%d" % i, [128, 512], F32) for i in range(2)]
        psO = [pp("psO%d" % i, [128, 512], F32) for i in range(2)]
        psG = pp("psG", [128, 512], F32)

        P = Prog()
        op = P.op

        WIN_KEYS = ["win%d" % k for k in range(8)]
        WOUT_KEYS = ["wout%d" % k for k in range(8)]
        KTAUG = ["KTaug%d" % g for g in range(4)]
        op("sp", lambda e: e.dma_start(out=ident[:], in_=id_d[:, :]), writes=["ident"], dma="c0")
        op("sp", lambda e: e.dma_start(out=maskT[:].rearrange("p a b -> p (a b)"), in_=mk_d[:, :]), writes=["maskT"], dma="c1")
        op("sp", lambda e: e.dma_start(out=pmT[:].rearrange("p a b -> p (a b)"), in_=pm_d[:, :]), writes=["pmT"], dma="c2")
        op("sp", lambda e: e.dma_start(out=ownT[:].rearrange("p a b -> p (a b)"), in_=ow_d[:, :]), writes=["ownT"], dma="c3")
        op("sp", lambda e: e.dma_start(out=fgbc[:], in_=fg_d.partition_broadcast(128)), writes=["fgbc"], dma="c5")
        op("dve", lambda e: e.memset(ones_bf[:], 1.0), writes=["ones_bf"])
        op("dve", lambda e: e.memset(Vcf[:], 0.0), writes=["Vones"])
        op("dve", lambda e: e.memset(Vcf[:, 0:NT * 4 * 65].rearrange("p (c d) -> p c d", d=65)[:, :, 64:65], 1.0), reads=["Vones"], writes=["Vones"])
        op("dve", lambda e: e.memset(KT[64:128, :, :], 0.0), writes=KTAUG + ["KTpad"])
        op("dve", lambda e: e.memset(QT[64:128, :, :], 0.0), writes=["QTa%d" % i for i in range(4)])
        for g in range(4):
            op("sp", lambda e, g=g: e.dma_start(out=KT[64:80, g, :], in_=ka_d[:, :]), writes=["KTaug%d" % g], dma="c4_%d" % g)
        op("dve", lambda e: e.memset(kmT[:], 0.0), writes=["kmT"])
        op("dve", lambda e: e.memset(e64[:], 0.0), writes=["e64"])
        op("dve", lambda e: e.memset(es_bf[:], 0.0), writes=["es_bf"])
        op("dve", lambda e: e.memset(orow[:], 0.0), writes=["orow"])
        op("dve", lambda e: e.memset(orow[0:1, :], 1.0), reads=["orow"], writes=["orow"])
        for i_ in range(4):
            op("dve", lambda e, i_=i_: e.memset(rr[i_][:], 0.0), writes=["rr%d" % i_])
        op("dve", lambda e: e.memset(e64[0:1, 64:65], 1.0), reads=["e64"], writes=["e64"])

        out_ops = []

        win_loaded = set()
        stepN_prefetched = set()

        def load_win(li2):
            L2 = layers[li2]
            b2 = (L2 % 2 == 1)
            j2 = L2 // 2
            IN2 = IN_B if b2 else IN_A
            wd2 = wb_d if b2 else wa_d
            for kc in range(8):
                op("pool", lambda e, kc=kc: e.dma_start(out=win[:, kc, 0:IN2], in_=wd2[j2, kc * 128:(kc + 1) * 128, :]),
                   writes=[WIN_KEYS[kc]], dma="wi%d" % kc)
            win_loaded.add(li2)

        def emit_layer(li, L):
            is_b = (L % 2 == 1)
            j = L // 2
            nkv = 4 if is_b else 2
            IN = IN_B if is_b else IN_A
            kvw = nkv * HD
            koff, voff, zoff = 1024, 1024 + kvw, 1024 + 2 * kvw
            w_d = wb_d if is_b else wa_d
            KK = 128
            src_d = x_d if li == 0 else hs_d
            dst_d = y_d if li == len(layers) - 1 else hs_d
            last = (li == len(layers) - 1)
            do_fn = last and final_norm

            if li not in win_loaded:
                load_win(li)
            def load_wout():
                for kc in range(8):
                    op("pool", lambda e, kc=kc: e.dma_start(out=wout[:, kc, :], in_=wo_d[L, kc * 128:(kc + 1) * 128, :]),
                       writes=[WOUT_KEYS[kc]], dma="wo%d" % kc)
            if li != 0:
                load_wout()
            if li not in stepN_prefetched:
                op("sp", lambda e: e.dma_start(out=gbc[:], in_=ng_d[L].partition_broadcast(128)), writes=["gbc"], dma="g")
            if not is_b:
                op("dve", lambda e: e.memset(QT[64:128, :, :], 0.0), writes=["QTa%d" % i for i in range(4)])
                op("sp", lambda e: e.dma_start(out=es_raw[:], in_=sk_d[j].partition_broadcast(128)), writes=["es_raw"], dma="sk")
                op("act", lambda e: e.activation(out=es_t[:], in_=es_raw[:], func=AF.Exp), reads=["es_raw"], writes=["es_t"])
                op("dve", lambda e: e.tensor_copy(out=es_bf[0:1, :], in_=es_t[0:1, :]), reads=["es_t"], writes=["es_bf"])

            def rms_stats(src, src_key, col):
                c = ss[:, col:col + 1]
                if is_b:
                    op("dve", lambda e: e.scalar_tensor_tensor(out=tAs[0][:].bitcast(BF16), in0=src, scalar=1.0, in1=src, op0=ALU.mult, op1=ALU.mult, accum_out=c),
                       reads=[src_key], writes=["tA0", "ss%d" % col])
                else:
                    op("act", lambda e: e.activation(out=tAs[0][:].bitcast(BF16), in_=src, func=AF.Square, accum_out=c),
                       reads=[src_key], writes=["tA0", "ss%d" % col])
                op("dve", lambda e: e.tensor_scalar(out=c, in0=c, scalar1=1.0 / D, scalar2=EPS, op0=ALU.mult, op1=ALU.add),
                   reads=["ss%d" % col], writes=["ss%d" % col])
                op("act", lambda e: e.activation(out=c, in_=c, func=AF.Ln), reads=["ss%d" % col], writes=["ss%d" % col])
                op("act", lambda e: e.activation(out=c, in_=c, func=AF.Exp, scale=-0.5), reads=["ss%d" % col], writes=["ss%d" % col])
                return c

            ipb = [0]

            def next_ip():
                ipb[0] ^= 1
                return ipb[0]

            rope_cnt = [0]

            def rope_tile(pk, ps, dst0, dst1, dk0, dk1):
                ri = rope_cnt[0] % 2
                rope_cnt[0] += 1
                tA, tB = tAs[ri], tBs[ri]
                ka = "tA%d" % ri
                kb_ = "tB%d_" % ri
                op("dve", lambda e: e.tensor_tensor(out=tA[:], in0=ps[:], in1=cosG[:], op=ALU.mult), reads=[pk, "cosG"], writes=[ka])
                for (o0, i0) in ((0, 32), (32, 0), (64, 96), (96, 64)):
                    op("dve", lambda e, o0=o0, i0=i0: e.tensor_tensor(out=tB[o0:o0 + 32, :], in0=ps[i0:i0 + 32, :], in1=t2G[o0:o0 + 32, :], op=ALU.mult),
                       reads=[pk, "t2G"], writes=[kb_ + str(o0)])
                op(ROPE_ADD_ENG, lambda e: e.tensor_tensor(out=dst0, in0=tA[0:64, :], in1=tB[0:64, :], op=ALU.add),
                   reads=[ka, kb_ + "0", kb_ + "32"], writes=[dk0])
                op(ROPE_ADD_ENG, lambda e: e.tensor_tensor(out=dst1, in0=tA[64:128, :], in1=tB[64:128, :], op=ALU.add),
                   reads=[ka, kb_ + "64", kb_ + "96"], writes=[dk1])

            def emit_stepN_dma(G, tt, src=None):
                t = 4 * G + tt
                sl = t % 2
                src = src_d if src is None else src
                op("sp", lambda e: e.dma_start(out=hl[sl][:], in_=src[t * 128:(t + 1) * 128, :]),
                   reads=["hd%d" % t], writes=["hl%d" % sl], dma="hl%d" % sl)

            def emit_stepN_a1(G, tt):
                t = 4 * G + tt
                sl = t % 2
                c = ss[:, 4 + tt:5 + tt]
                sk = "ssN%d" % tt
                src = hl[sl][:]
                if is_b:
                    op("dve", lambda e: e.scalar_tensor_tensor(out=tAs[0][:].bitcast(BF16), in0=src, scalar=1.0, in1=src, op0=ALU.mult, op1=ALU.mult, accum_out=c),
                       reads=["hl%d" % sl], writes=["tA0", sk])
                else:
                    op("act", lambda e: e.activation(out=tAs[0][:].bitcast(BF16), in_=src, func=AF.Square, accum_out=c),
                       reads=["hl%d" % sl], writes=["tA0", sk])
                op("dve", lambda e: e.tensor_scalar(out=c, in0=c, scalar1=1.0 / D, scalar2=EPS, op0=ALU.mult, op1=ALU.add),
                   reads=[sk], writes=[sk])

            def emit_stepN_a2(G, tt):
                c = ss[:, 4 + tt:5 + tt]
                sk = "ssN%d" % tt
                op("act", lambda e: e.activation(out=c, in_=c, func=AF.Ln), reads=[sk], writes=[sk])
                op("act", lambda e: e.activation(out=c, in_=c, func=AF.Exp, scale=-0.5), reads=[sk], writes=[sk])

            def emit_stepN_a3(G, tt):
                t = 4 * G + tt
                sl = t % 2
                c = ss[:, 4 + tt:5 + tt]
                op("dve", lambda e: e.scalar_tensor_tensor(out=hn[sl][:], in0=hl[sl][:], scalar=c, in1=gbc[:], op0=ALU.mult, op1=ALU.mult),
                   reads=["hl%d" % sl, "ssN%d" % tt, "gbc"], writes=["hn%d" % sl])

            def emit_stepN_a(G, tt):
                emit_stepN_dma(G, tt)
                emit_stepN_a1(G, tt)
                emit_stepN_a2(G, tt)
                emit_stepN_a3(G, tt)

            def emit_stepN_b(G, tt):
                t = 4 * G + tt
                sl = t % 2
                for kc in range(8):
                    op("pe", lambda e, kc=kc: e.transpose(out=psT[:, kc * 128:(kc + 1) * 128], in_=hn[sl][:, kc * 128:(kc + 1) * 128], identity=ident[:]),
                       reads=["hn%d" % sl, "ident"], writes=["psT"])
                if is_b:
                    op("dve", lambda e: e.tensor_copy(out=hnT[:, :, tt * 128:(tt + 1) * 128], in_=psT[:].rearrange("p (k t) -> p k t", k=8)),
                       reads=["psT"], writes=["hnT%d" % tt])
                else:
                    op("act", lambda e: e.activation(out=hnT[:, :, tt * 128:(tt + 1) * 128], in_=psT[:].rearrange("p (k t) -> p k t", k=8), func=AF.Copy),
                       reads=["psT"], writes=["hnT%d" % tt])

            def emit_stepN_tile(G, tt):
                emit_stepN_a(G, tt)
                emit_stepN_b(G, tt)

            def emit_group(G):
                tok0 = 512 * G
                HNT = ["hnT%d" % tt for tt in range(4)]
                op("sp", lambda e, tok0=tok0: e.dma_start(out=cosG[:], in_=cos_d[:, tok0:tok0 + 512]), writes=["cosG"], dma="cos")
                op("sp", lambda e, tok0=tok0: e.dma_start(out=t2G[:], in_=t2_d[:, tok0:tok0 + 512]), writes=["t2G"], dma="t2")

                IPB = {"psIP0": psIP[0], "psIP1": psIP[1], "psS0": psS[0], "psS1": psS[1]}
                rb = [0]
                zb = [0]

                def proj_fm(col0, bk):
                    for kc in range(8):
                        op("pe", lambda e, kc=kc: e.matmul(IPB[bk][:], lhsT=win[:, kc, col0:col0 + 128], rhs=hnT[:, kc, :], start=(kc == 0), stop=(kc == 7)),
                           reads=WIN_KEYS + HNT, writes=[bk])

                def rope_bank():
                    rb[0] ^= 1
                    return "psIP%d" % rb[0]

                def z_bank():
                    zb[0] ^= 1
                    return "psS%d" % zb[0]

                for f in range(8):
                    bk = rope_bank()
                    proj_fm(f * 128, bk)
                    rope_tile(bk, IPB[bk], QT[0:64, 2 * f, :], QT[0:64, 2 * f + 1, :], "QT%d" % (2 * f), "QT%d" % (2 * f + 1))
                    if f == 0 and tail_box[0] is not None:
                        tail_box[0]()
                        tail_box[0] = None
                    bz = z_bank()
                    proj_fm(zoff + f * 128, bz)
                    op("act", lambda e, bz=bz, f=f: e.activation(out=szT[:, f, :], in_=IPB[bz][:], func=AF.Silu),
                       reads=[bz], writes=["szT%d" % f])
                    if f < nkv // 2:
                        kt = f
                        bk = rope_bank()
                        proj_fm(koff + kt * 128, bk)
                        rope_tile(bk, IPB[bk], KT[0:64, 2 * kt, tok0:tok0 + 512], KT[0:64, 2 * kt + 1, tok0:tok0 + 512],
                                  "KT%d_%d" % (2 * kt, G), "KT%d_%d" % (2 * kt + 1, G))
                    if f < 4:
                        tt = f
                        t = 4 * G + tt
                        bz = z_bank()
                        for kc in range(8):
                            op("pe", lambda e, bz=bz, kc=kc, tt=tt: e.matmul(IPB[bz][:, 0:kvw], lhsT=hnT[:, kc, tt * 128:(tt + 1) * 128], rhs=win[:, kc, voff:voff + kvw], start=(kc == 0), stop=(kc == 7)),
                               reads=WIN_KEYS + HNT, writes=[bz])
                        op("dve", lambda e, bz=bz, t=t: e.tensor_copy(out=Vc[:, t, 0:nkv, 0:64], in_=IPB[bz][:, 0:kvw].rearrange("p (k d) -> p k d", k=nkv)),
                           reads=[bz], writes=["V%d" % t])
                def stepG_a0(tt):
                    t = 4 * G + tt
                    qb = t // 2
                    qs = slice(tt * 128, (tt + 1) * 128)
                    QTK = ["QT%d" % h for h in range(NH)]
                    for h in range(NH):
                        op("pe", lambda e, h=h: e.matmul(psG[:, h * 16:(h + 1) * 16], lhsT=QT[0:64, h, qs], rhs=kmT[:, h // 4, :], start=True, stop=True),
                           reads=QTK + ["kmT"], writes=["psG"])
                    op("dve", lambda e: e.tensor_tensor(out=gm[:], in0=psG[:, 0:256].rearrange("p (h j) -> p h j", h=NH),
                                                        in1=pmT[:, qb:qb + 1, :].to_broadcast([128, NH, 16]), op=ALU.add),
                       reads=["psG", "pmT"], writes=["gm"])

                def stepG_max(tt, h0):
                    for h in range(h0, h0 + 4):
                        op("dve", lambda e, h=h: e.max(out=m8[:, h, :], in_=gm[:, h, :]), reads=["gm"], writes=["m8_%d" % h])

                def stepG_sel(tt):
                    op("dve", lambda e: e.tensor_tensor(out=sel[:], in0=gm[:], in1=m8[:, :, 2:3].to_broadcast([128, NH, 16]), op=ALU.is_ge),
                       reads=["gm"] + ["m8_%d" % h for h in range(NH)], writes=["sel"])

                def stepG_bias(tt):
                    qb = (4 * G + tt) // 2
                    op("dve", lambda e: e.scalar_tensor_tensor(out=biasb[:], in0=sel[:], scalar=-1.0, in1=ownT[:, qb:qb + 1, :].to_broadcast([128, NH, 16]),
                                                               op0=ALU.add, op1=ALU.add),
                       reads=["sel", "ownT"], writes=["biasb"])

                def stepG_b(tt, half):
                    qs = slice(tt * 128, (tt + 1) * 128)
                    for hh in range(8):
                        op("pe", lambda e, hh=hh: e.transpose(out=psT[0:16, hh * 128:(hh + 1) * 128], in_=biasb[:, half * 8 + hh, :], identity=ident[:]),
                           reads=["biasb", "ident"], writes=["psT"])
                    op("dve", lambda e: e.tensor_copy(out=QT[64:80, half * 8:(half + 1) * 8, qs], in_=psT[0:16, :].rearrange("p (h q) -> p h q", h=8)),
                       reads=["psT"], writes=["QTa%d" % tt])

                def emit_stepG_a(tt):
                    qb = (4 * G + tt) // 2
                    qs = slice(tt * 128, (tt + 1) * 128)
                    if qb < 4:
                        op("dve", lambda e: e.memset(QT[64:80, :, qs], 0.0), writes=["QTa%d" % tt])
                        return False
                    stepG_a0(tt)
                    for h0 in range(0, NH, 4):
                        stepG_max(tt, h0)
                    stepG_sel(tt)
                    stepG_bias(tt)
                    return True

                def emit_stepG_b(tt):
                    stepG_b(tt, 0)
                    stepG_b(tt, 1)

                def emit_stepG_tile(tt):
                    if emit_stepG_a(tt):
                        emit_stepG_b(tt)

                if li == 0 and G == 0:
                    load_wout()
                if G == DEBUG_NG - 1 and li + 1 < len(layers):
                    load_win(li + 1)
                if is_b:
                    KTG = ["KT%d_%d" % (g, G) for g in range(4)]
                    op("dve", lambda e, tok0=tok0: e.tensor_reduce(out=kms[:], in_=KT[0:64, :, tok0:tok0 + 512].rearrange("p k (b s) -> p k b s", b=2), axis=AX.X, op=ALU.add),
                       reads=KTG, writes=["kms"])
                    op("dve", lambda e, G=G: e.tensor_scalar(out=kmT[:, :, 2 * G:2 * G + 2], in0=kms[:], scalar1=1.0 / 256, scalar2=None, op0=ALU.mult),
                       reads=["kms"], writes=["kmT"])
                    emit_stepG_tile(0)
                    if 4 * G // 2 < 4:
                        for tt_ in range(1, 4):
                            emit_stepG_tile(tt_)
                units = []
                for tt in range(4):
                    t = 4 * G + tt
                    if is_b:
                        chunks = [(c, None) for c in range(t)] + [(t, 0)]
                    else:
                        chunks = ([(t - 1, 1)] if t >= 1 else []) + [(t, 0)]
                    for g4 in range(4):
                        for i, (c, mk) in enumerate(chunks):
                            units.append((tt, g4, i, c, mk, i == len(chunks) - 1))
                nu = len(units)
                SB = [psS[0], psS[1], psIP[1]]
                SBK = ["psS0", "psS1", "psIP1"]
                LA = 2
                pending = []

                def defer(due, chain, stage, fn):
                    pending.append((due, chain, stage, fn))

                def run_pending(u, max_stage=10, min_stage=0, max_chain=1 << 30):
                    while True:
                        cand = [p for p in pending if p[0] <= u and min_stage <= p[2] <= max_stage and p[1] <= max_chain]
                        if not cand:
                            return
                        cand.sort(key=lambda p: (p[1], p[2]))
                        pending.remove(cand[0])
                        cand[0][3]()

                DL1, DL2, DL3 = (5, 7, 10) if is_b else (3, 5, 8)
                first_unit = {}
                for ui, un in enumerate(units):
                    first_unit.setdefault(un[0], ui)
                nxt_layer = (G + 1 == DEBUG_NG) and (li + 1 < len(layers))
                if nxt_layer:
                    op("sp", lambda e: e.dma_start(out=gbc[:], in_=ng_d[layers[li + 1]].partition_broadcast(128)), writes=["gbc"], dma="g")
                    stepN_prefetched.add(li + 1)
                if G + 1 < DEBUG_NG or nxt_layer:
                    Gn = (G + 1) if not nxt_layer else 0
                    srcn = None if not nxt_layer else hs_d
                    for tt_ in range(4):
                        if tt_ == 0:
                            ddue = 0
                        elif tt_ == 1:
                            ddue = 1
                        else:
                            ddue = max(first_unit[tt_ - 1] + 1, first_unit[tt_ - 2] + 7)
                        defer(ddue, tt_ * 4 - 3.9, 0, lambda tt_=tt_: emit_stepN_dma(Gn, tt_, srcn))
                        defer(first_unit[tt_] + 1, tt_ * 4 + 0.25, 0, lambda tt_=tt_: emit_stepN_a1(Gn, tt_))
                        defer(first_unit[tt_] + 4, tt_ * 4 + 0.26, 0, lambda tt_=tt_: emit_stepN_a2(Gn, tt_))
                        defer(first_unit[tt_] + 6, tt_ * 4 + 0.27, 0, lambda tt_=tt_: emit_stepN_a3(Gn, tt_))
                        defer(first_unit[tt_] + 10, tt_ * 4 + 0.75, 0, lambda tt_=tt_: emit_stepN_b(Gn, tt_))
                if is_b and (4 * G) // 2 >= 4:
                    for tt_ in range(3):
                        fu = first_unit[tt_]
                        defer(fu + 3, tt_ * 4 + 0.50, 0, lambda tt_=tt_: stepG_a0(tt_ + 1))
                        for i_ in range(4):
                            defer(fu + 5 + i_, tt_ * 4 + 0.51 + 0.01 * i_, 0, lambda tt_=tt_, i_=i_: stepG_max(tt_ + 1, 4 * i_))
                        defer(fu + 9, tt_ * 4 + 0.56, 0, lambda tt_=tt_: stepG_sel(tt_ + 1))
                        defer(fu + 10, tt_ * 4 + 0.57, 0, lambda tt_=tt_: stepG_bias(tt_ + 1))
                        defer(fu + 14, tt_ * 4 + 0.60, 0, lambda tt_=tt_: stepG_b(tt_ + 1, 0))
                        defer(fu + 17, tt_ * 4 + 0.61, 0, lambda tt_=tt_: stepG_b(tt_ + 1, 1))

                def emit_qk(u):
                    tt, g4, i, c, mk, lastc = units[u]
                    kvh = g4 if is_b else g4 // 2
                    sbk = u % 3
                    qs = slice(tt * 128, (tt + 1) * 128)
                    rd = ["KT%d_%d" % (kvh, c // 4)] + ["QT%d" % h for h in range(4 * g4, 4 * g4 + 4)]
                    rd += ["QTa%d" % tt] + KTAUG
                    op("pe", lambda e: e.matmul(SB[sbk][:], lhsT=KT[0:KK, kvh, c * 128:(c + 1) * 128], rhs=QT[0:KK, 4 * g4:4 * g4 + 4, qs], start=True, stop=(mk is None)),
                       reads=rd, writes=[SBK[sbk]])
                    if mk is not None:
                        op("pe", lambda e: e.matmul(SB[sbk][:], lhsT=ident[:], rhs=maskT[:, mk, :], start=False, stop=True),
                           reads=["ident", "maskT"], writes=[SBK[sbk]])

                def emit_exp_pv(u):
                    tt, g4, i, c, mk, lastc = units[u]
                    kvh = g4 if is_b else g4 // 2
                    sbk = u % 3
                    ps_ = u % NPT
                    ob = (tt * 4 + g4) % 2
                    op("act", lambda e: e.activation(out=pT[ps_][:], in_=SB[sbk][:], func=AF.Exp, scale=0.125),
                       reads=[SBK[sbk]], writes=["pT%d" % ps_])
                    op("pe", lambda e: e.matmul(psO[ob][:, :], lhsT=Vcf[:, (c * 4 + kvh) * 65:(c * 4 + kvh) * 65 + 128], rhs=pT[ps_][:], start=(i == 0), stop=(lastc and is_b)),
                       reads=["V%d" % c, "Vones", "pT%d" % ps_], writes=["psO%d" % ob])
                    if lastc and not is_b:
                        op("pe", lambda e: e.matmul(psO[ob][:, :], lhsT=e64[:, :], rhs=es_bf[:, 4 * g4:4 * g4 + 4].unsqueeze(2).to_broadcast([128, 4, 128]), start=False, stop=True),
                           reads=["e64", "es_bf"], writes=["psO%d" % ob])
                    if lastc:
                        ok = "psO%d" % ob
                        chain = tt * 4 + g4
                        kb = chain % NOSB
                        run_pending(1 << 30, max_stage=2, min_stage=1, max_chain=chain - NOSB)
                        op("dve", lambda e: e.tensor_copy(out=osb[kb][0:65, :], in_=psO[ob][0:65, :]), reads=[ok], writes=["osb%d" % kb])
                        defer(u + DL1, chain, 1, lambda: norm_stage1(u, tt, g4))

                def norm_stage1(u0, tt, g4):
                    kb = (tt * 4 + g4) % NOSB
                    op("act", lambda e: e.activation(out=nrm[32:33, :], in_=osb[kb][64:65, :], func=AF.Ln), reads=["osb%d" % kb], writes=["nrm_l"])
                    op("act", lambda e: e.activation(out=rr[kb][0:1, :], in_=nrm[32:33, :], func=AF.Exp, scale=-1.0), reads=["nrm_l"], writes=["rr%d" % kb])
                    defer(u0 + DL2, tt * 4 + g4, 2, lambda: norm_stage2(u0, tt, g4))

                def norm_stage2(u0, tt, g4):
                    qs = slice(tt * 128, (tt + 1) * 128)
                    gi = (4 * G + tt) % 2
                    kb = (tt * 4 + g4) % NOSB
                    op("pe", lambda e: e.matmul(psG[:], lhsT=orow[:, :], rhs=rr[kb][:, :], start=True, stop=True),
                       reads=["orow", "rr%d" % kb], writes=["psG"])
                    for par in range(2):
                        p0 = 64 * par
                        sz_in = szT[p0:p0 + 64, 2 * g4:2 * g4 + 2, qs]
                        rb_in = psG[p0:p0 + 64, :].rearrange("p (h q) -> p h q", h=4)[:, par::2, :]
                        o_in = osb[kb][0:64, :].rearrange("p (h q) -> p h q", h=4)[:, par::2, :]
                        op("dve", lambda e, sz_in=sz_in, rb_in=rb_in, par=par: e.tensor_tensor(out=szr[par][0:64, :, :], in0=sz_in, in1=rb_in, op=ALU.mult),
                           reads=["szT%d" % (2 * g4), "szT%d" % (2 * g4 + 1), "psG"], writes=["szr%d" % par])
                        op("dve", lambda e, o_in=o_in, par=par, p0=p0: e.tensor_tensor(out=gatedT[gi][p0:p0 + 64, 2 * g4:2 * g4 + 2, :], in0=o_in, in1=szr[par][0:64, :, :], op=ALU.mult),
                           reads=["osb%d" % kb, "szr%d" % par], writes=["gT%d_%d_%d" % (gi, g4, par)])
                    if g4 == 3:
                        defer(u0 + DL3, tt * 4 + g4, 3, lambda: emit_outproj(tt, u0 + DL3))

                def emit_outproj(tt, ub):
                    t = 4 * G + tt
                    gi = t % 2
                    sl = t % 2
                    chain = tt * 4 + 3
                    GT = ["gT%d_%d_%d" % (gi, g4, par) for g4 in range(4) for par in range(2)]
                    op("sp", lambda e: e.dma_start(out=hr[sl][:], in_=src_d[t * 128:(t + 1) * 128, :]),
                       reads=["hd%d" % t], writes=["hr%d" % sl], dma="hr%d" % sl)

                    def piece(n, f0):
                        for f in (f0, f0 + 1):
                            op("pe", lambda e, f=f: e.matmul(psIP[0][:], lhsT=gatedT[gi][:, f, :], rhs=wout[:, f, n * 512:(n + 1) * 512], start=(f == 0), stop=(f == 7)),
                               reads=GT + WOUT_KEYS, writes=["psIP0"])

                    def add_half(n):
                        op("dve", lambda e: e.tensor_tensor(out=hr[sl][:, n * 512:(n + 1) * 512], in0=psIP[0][:], in1=hr[sl][:, n * 512:(n + 1) * 512], op=ALU.add),
                           reads=["psIP0", "hr%d" % sl], writes=["hr%d" % sl])
                        if n == 1:
                            finish()

                    def finish():
                        if do_fn:
                            c = rms_stats(hr[sl][:], "hr%d" % sl, 1)
                            op("dve", lambda e, c=c: e.scalar_tensor_tensor(out=hr[sl][:], in0=hr[sl][:], scalar=c, in1=fgbc[:], op0=ALU.mult, op1=ALU.mult),
                               reads=["hr%d" % sl, "ss1", "fgbc"], writes=["hr%d" % sl])
                        o = op("pool", lambda e: e.dma_start(out=dst_d[t * 128:(t + 1) * 128, :], in_=hr[sl][:]),
                               reads=["hr%d" % sl], writes=["hd%d" % t], dma="st%d" % sl)
                        if last:
                            out_ops.append(o)

                    n_next = sum(1 for un in units if un[0] == tt + 1)
                    if n_next >= 14:
                        for n in range(2):
                            base = ub + n * 6
                            for k in range(4):
                                defer(base + k, chain, 3.0 + 0.01 * (n * 6 + k + 1), lambda n=n, k=k: piece(n, 2 * k))
                            defer(base + 5, chain, 3.0 + 0.01 * (n * 6 + 6), lambda n=n: add_half(n))
                    else:
                        OPB = [(psIP[0][:], "psIP0"), (psT[:].bitcast(F32), "psT")]
                        for n in range(2):
                            for f in range(8):
                                op("pe", lambda e, n=n, f=f: e.matmul(OPB[n][0], lhsT=gatedT[gi][:, f, :], rhs=wout[:, f, n * 512:(n + 1) * 512], start=(f == 0), stop=(f == 7)),
                                   reads=GT + WOUT_KEYS, writes=[OPB[n][1]])
                        for n in range(2):
                            op("dve", lambda e, n=n: e.tensor_tensor(out=hr[sl][:, n * 512:(n + 1) * 512], in0=OPB[n][0], in1=hr[sl][:, n * 512:(n + 1) * 512], op=ALU.add),
                               reads=[OPB[n][1], "hr%d" % sl], writes=["hr%d" % sl])
                        finish()

                for v in range(min(LA, nu)):
                    emit_qk(v)
                for u in range(nu):
                    if u + LA < nu:
                        emit_qk(u + LA)
                    emit_exp_pv(u)
                    run_pending(u)
                run_pending(1 << 30, max_stage=0.9)

                def flush_tail():
                    while pending:
                        run_pending(1 << 30)
                if G + 1 < DEBUG_NG:
                    tail_box[0] = flush_tail
                else:
                    flush_tail()

            tail_box = [None]
            if li not in stepN_prefetched:
                for tt in range(4):
                    emit_stepN_tile(0, tt)
            for G in range(DEBUG_NG):
                emit_group(G)

        for li, L in enumerate(layers):
            emit_layer(li, L)

        P.emit(nc, final_wait_ops=out_ops)
    return nc


_CONSTS = None
_PROGS = {}


def _get_prog(layers, final_norm):
    key = (tuple(layers), final_norm)
    if key not in _PROGS:
        _PROGS[key] = build_program(list(layers), final_norm)
    return _PROGS[key]


LAUNCH_PLAN = [([0, 1, 2, 3], True)]


def kernel(x, norm_g, w_in_a, sinks_a, w_in_b, w_out, final_g):
    global _CONSTS
    if _CONSTS is None:
        _CONSTS = _host_consts()
    n = 8
    x = np.ascontiguousarray(np.asarray(x, dtype=np.float32))
    shared = {"norm_g": np.ascontiguousarray(np.asarray(norm_g, np.float32)),
              "w_in_a": np.ascontiguousarray(np.asarray(w_in_a, np.float32)),
              "sinks_a": np.ascontiguousarray(np.asarray(sinks_a, np.float32)),
              "w_in_b": np.ascontiguousarray(np.asarray(w_in_b, np.float32)),
              "w_out": np.ascontiguousarray(np.asarray(w_out, np.float32)),
              "final_g": np.ascontiguousarray(np.asarray(final_g, np.float32))}
    shared.update(_CONSTS)
    cur = [x[b] for b in range(n)]
    for layers, fn in LAUNCH_PLAN:
        nc = _get_prog(layers, fn)
        in_maps = [dict(shared, x=cur[b]) for b in range(n)]
        res = run_bass_kernel_spmd(nc, in_maps, core_ids=list(range(n)))
        cur = [np.asarray(r["y"]) for r in res.results]
    return np.stack(cur, 0).astype(np.float32)
```
